# Optimizing a Trainium2 kernel written in Bass

```python
import math
import numpy as np
import jax
import jax.numpy as jnp
from jax import lax

D_MODEL = 4096
BATCH = 2
SEQ = 8192
DEPTH = 2

N_META = 16
GROUP_WIDTH = D_MODEL // 2
D_FF = 4 * D_MODEL
NORM_EPS = 1e-6
Q_BLOCK = 128
ROPE_THETA = 500000.0

LRU_WIDTH = GROUP_WIDTH
LRU_BLOCKS = 16
LRU_BLOCK_DIM = LRU_WIDTH // LRU_BLOCKS
CONV_WIDTH = 4
LRU_C = 8.0

DIFF_HEAD_DIM = 128
DIFF_HEADS = GROUP_WIDTH // (2 * DIFF_HEAD_DIM)
ROT_DIM = DIFF_HEAD_DIM // 4

RWKV_HEAD_DIM = 64
RWKV_HEADS = GROUP_WIDTH // RWKV_HEAD_DIM
DECAY_LORA = max(32, round(D_MODEL ** 0.5 * 1.8 / 32) * 32)
AAA_LORA = max(32, round(D_MODEL ** 0.5 * 1.8 / 32) * 32)
GATE_LORA = max(32, round(D_MODEL ** 0.8 * 0.6 / 32) * 32)
RWKV_GN_EPS = 64e-5

FOX_HEAD_DIM = 128
FOX_HEADS = GROUP_WIDTH // FOX_HEAD_DIM

EVEN_SIZES = (LRU_WIDTH, LRU_WIDTH, 2 * DIFF_HEADS * DIFF_HEAD_DIM,
              2 * DIFF_HEADS * DIFF_HEAD_DIM, 2 * DIFF_HEADS * DIFF_HEAD_DIM)
RWKV_SIZES = (GROUP_WIDTH, GROUP_WIDTH, GROUP_WIDTH, DECAY_LORA, AAA_LORA, GATE_LORA)
RWKV_SLAB = sum(RWKV_SIZES)
ODD_SIZES = (RWKV_SLAB, FOX_HEADS * FOX_HEAD_DIM, FOX_HEADS * FOX_HEAD_DIM,
             FOX_HEADS * FOX_HEAD_DIM, FOX_HEADS)

F32 = jnp.float32

kernel_name = "hybrid_lru_diffattn_rwkv7_fox_trunk"


def _split(u, sizes):
    cuts = [int(c) for c in np.cumsum(sizes)[:-1]]
    return jnp.split(u, cuts, axis=-1)


def _rms_norm(x, gain, eps=NORM_EPS):
    x32 = x.astype(F32)
    y = x32 * lax.rsqrt(jnp.mean(x32 * x32, axis=-1, keepdims=True) + eps)
    return (y * gain.astype(F32)).astype(x.dtype)


def _partial_rotary(x, pos):
    half = ROT_DIM // 2
    inv_freq = ROPE_THETA ** (-jnp.arange(half, dtype=F32) / half)
    ang = pos.astype(F32)[:, None] * inv_freq[None, :]
    bshape = (1, pos.shape[0]) + (1,) * (x.ndim - 3) + (half,)
    cos = jnp.cos(ang).reshape(bshape)
    sin = jnp.sin(ang).reshape(bshape)
    xr = x[..., :ROT_DIM].astype(F32)
    x1, x2 = xr[..., :half], xr[..., half:]
    rot = jnp.concatenate([x1 * cos - x2 * sin, x2 * cos + x1 * sin], axis=-1).astype(x.dtype)
    return jnp.concatenate([rot, x[..., ROT_DIM:]], axis=-1)


def _causal_block_sweep(attend, q_args):
    B, T = q_args[0].shape[:2]
    n_blk = (T - N_META) // Q_BLOCK
    meta_out = attend(tuple(q[:, :N_META] for q in q_args), jnp.arange(N_META))

    def to_blocks(q):
        r = q[:, N_META:].reshape((B, n_blk, Q_BLOCK) + q.shape[2:])
        return jnp.moveaxis(r, 1, 0)

    blocks = tuple(to_blocks(q) for q in q_args)
    starts = N_META + Q_BLOCK * jnp.arange(n_blk)
    out = lax.map(lambda bs: attend(bs[0], bs[1] + jnp.arange(Q_BLOCK)), (blocks, starts))
    out = jnp.moveaxis(out, 0, 1).reshape((B, n_blk * Q_BLOCK) + out.shape[3:])
    return jnp.concatenate([meta_out, out], axis=1)


def _linear_combine(left, right):
    a1, b1 = left
    a2, b2 = right
    return a1 * a2, a2 * b1 + b2


def _rglru_branch(xb, gb, conv_w, conv_b, w_a, b_a, w_x, b_x, lam):
    B, T, W = xb.shape
    xp = jnp.pad(xb, ((0, 0), (CONV_WIDTH - 1, 0), (0, 0)))
    xc = conv_b + sum(xp[:, j:j + T] * conv_w[j] for j in range(CONV_WIDTH))
    xh = xc.reshape(B, T, LRU_BLOCKS, LRU_BLOCK_DIM)
    r = jax.nn.sigmoid((jnp.einsum('btnc,ncd->btnd', xh, w_a) + b_a).astype(F32)).reshape(B, T, W)
    i = jax.nn.sigmoid((jnp.einsum('btnc,ncd->btnd', xh, w_x) + b_x).astype(F32)).reshape(B, T, W)
    log_a = -LRU_C * r * jax.nn.softplus(-lam.astype(F32))
    a = jnp.exp(log_a)
    u = jnp.sqrt(-jnp.expm1(2.0 * log_a)) * (i * xc.astype(F32))
    _, h = lax.associative_scan(_linear_combine, (a, u), axis=1)
    return h.astype(xb.dtype) * jax.nn.gelu(gb)


def _diff_lambda_init(layer):
    return 0.8 - 0.6 * math.exp(-0.3 * layer)


def _diff_attention_branch(q, k, v, pos, q_gain, k_gain, lq1, lk1, lq2, lk2, sub_gain, lambda_init):
    B, T, _ = q.shape
    q = _partial_rotary(_rms_norm(q.reshape(B, T, DIFF_HEADS, 2, DIFF_HEAD_DIM), q_gain), pos)
    k = _partial_rotary(_rms_norm(k.reshape(B, T, DIFF_HEADS, 2, DIFF_HEAD_DIM), k_gain), pos)
    v = v.reshape(B, T, DIFF_HEADS, 2 * DIFF_HEAD_DIM)
    k1, k2 = k[..., 0, :], k[..., 1, :]
    lam = (jnp.exp(jnp.sum(lq1.astype(F32) * lk1.astype(F32)))
           - jnp.exp(jnp.sum(lq2.astype(F32) * lk2.astype(F32))) + lambda_init)
    scale = DIFF_HEAD_DIM ** -0.5
    kpos = jnp.arange(T)

    def attend(qb, qpos):
        q1b, q2b = qb
        mask = kpos[None, :] <= qpos[:, None]

        def probs(qx, kx):
            s = jnp.einsum('bqhd,bkhd->bhqk', qx, kx).astype(F32) * scale
            return jax.nn.softmax(jnp.where(mask, s, -jnp.inf), axis=-1)

        p = probs(q1b, k1) - lam * probs(q2b, k2)
        return jnp.einsum('bhqk,bkhe->bqhe', p.astype(v.dtype), v)

    o = _causal_block_sweep(attend, (q[..., 0, :], q[..., 1, :]))
    o = _rms_norm(o, sub_gain) * (1.0 - lambda_init)
    return o.reshape(B, T, GROUP_WIDTH)


def _rwkv7_step(S, inp):
    r_t, w_t, k_t, v_t, a_t, b_t = inp
    sa = jnp.einsum('bhij,bhj->bhi', S, a_t)
    S = S * w_t[:, :, None, :] + sa[..., None] * b_t[:, :, None, :] + v_t[..., None] * k_t[:, :, None, :]
    y = jnp.einsum('bhij,bhj->bhi', S, r_t)
    return S, y


def _rwkv7_branch(z, mu, w0, w_up, a0, a_up, g_up, k_k, k_a, r_k, gn_w, gn_b):
    B, T, _ = z.shape
    z_prev = jnp.pad(z, ((0, 0), (1, 0), (0, 0)))[:, :T]
    z = z + (z_prev - z) * mu
    r, k, v, wd, ad, gd = _split(z, RWKV_SIZES)

    def heads(t):
        return t.reshape(B, T, RWKV_HEADS, RWKV_HEAD_DIM)

    w_log = -jax.nn.softplus(-(w0 + jnp.tanh(wd) @ w_up).astype(F32)) - 0.5
    decay = jnp.exp(-jnp.exp(w_log))
    a = jax.nn.sigmoid((a0 + ad @ a_up).astype(F32))
    g = jax.nn.sigmoid(gd) @ g_up
    kk = heads((k * k_k).astype(F32))
    kk = kk / jnp.maximum(jnp.linalg.norm(kk, axis=-1, keepdims=True), 1e-12)
    k32 = heads(k.astype(F32) * (1.0 + (a - 1.0) * k_a.astype(F32)))
    r32 = heads(r.astype(F32))
    v32 = heads(v.astype(F32))
    a_h = heads(a)

    def tm(t):
        return jnp.moveaxis(t, 1, 0)

    xs = (tm(r32), tm(heads(decay)), tm(k32), tm(v32), tm(-kk), tm(kk * a_h))
    S0 = jnp.zeros((B, RWKV_HEADS, RWKV_HEAD_DIM, RWKV_HEAD_DIM), F32)
    _, y = lax.scan(_rwkv7_step, S0, xs)
    y = jnp.moveaxis(y, 0, 1)
    mean = jnp.mean(y, axis=-1, keepdims=True)
    var = jnp.mean(jnp.square(y - mean), axis=-1, keepdims=True)
    y = ((y - mean) * lax.rsqrt(var + RWKV_GN_EPS)).reshape(B, T, GROUP_WIDTH) * gn_w + gn_b
    bonus = jnp.sum(r32 * k32 * r_k.astype(F32), axis=-1, keepdims=True) * v32
    out = (y + bonus.reshape(B, T, GROUP_WIDTH)) * g.astype(F32)
    return out.astype(z.dtype)


def _forgetting_attention_branch(q, k, v, f_logit, f_bias, q_gain, k_gain):
    B, T, _ = q.shape
    q = _rms_norm(q.reshape(B, T, FOX_HEADS, FOX_HEAD_DIM), q_gain)
    k = _rms_norm(k.reshape(B, T, FOX_HEADS, FOX_HEAD_DIM), k_gain)
    v = v.reshape(B, T, FOX_HEADS, FOX_HEAD_DIM)
    log_f = jax.nn.log_sigmoid((f_logit + f_bias).astype(F32))
    c = jnp.cumsum(log_f, axis=1)
    c_k = jnp.transpose(c, (0, 2, 1))
    scale = FOX_HEAD_DIM ** -0.5
    kpos = jnp.arange(T)

    def attend(qb, qpos):
        q_b, c_q = qb
        mask = kpos[None, :] <= qpos[:, None]
        s = (jnp.einsum('bqhd,bkhd->bhqk', q_b, k).astype(F32) * scale
             + jnp.transpose(c_q, (0, 2, 1))[..., None] - c_k[:, :, None, :])
        p = jax.nn.softmax(jnp.where(mask, s, -jnp.inf), axis=-1)
        return jnp.einsum('bhqk,bkhe->bqhe', p.astype(v.dtype), v)

    o = _causal_block_sweep(attend, (q, c))
    return o.reshape(B, T, GROUP_WIDTH)


def _even_mixer(hn, pos, lambda_init, w_in, conv_w, conv_b, lru_wa, lru_ba, lru_wx, lru_bx, lru_lam,
                q_gain, k_gain, lq1, lk1, lq2, lk2, sub_gain, w_out):
    xb, gb, q, k, v = _split(hn @ w_in, EVEN_SIZES)
    y_a = _rglru_branch(xb, gb, conv_w, conv_b, lru_wa, lru_ba, lru_wx, lru_bx, lru_lam)
    y_b = _diff_attention_branch(q, k, v, pos, q_gain, k_gain, lq1, lk1, lq2, lk2, sub_gain, lambda_init)
    return jnp.concatenate([y_a, y_b], axis=-1) @ w_out


def _odd_mixer(hn, w_in, mu, w0, w_up, a0, a_up, g_up, k_k, k_a, r_k, gn_w, gn_b,
               fox_q_gain, fox_k_gain, fox_f_bias, w_out):
    z, q, k, v, f = _split(hn @ w_in, ODD_SIZES)
    y_c = _rwkv7_branch(z, mu, w0, w_up, a0, a_up, g_up, k_k, k_a, r_k, gn_w, gn_b)
    y_d = _forgetting_attention_branch(q, k, v, f, fox_f_bias, fox_q_gain, fox_k_gain)
    return jnp.concatenate([y_c, y_d], axis=-1) @ w_out


def _squared_relu_mlp(hn, w_up, w_down):
    return jnp.square(jax.nn.relu(hn @ w_up)) @ w_down


def setup_inputs(seed: int = 0) -> dict:
    key = jax.random.key(seed)
    ks = iter(jax.random.split(key, 48))
    d = D_MODEL

    def nrm(shape, scale):
        return jax.random.normal(next(ks), shape, F32) * scale

    def gain(n):
        return 1.0 + nrm((n,), 0.02)

    def unif(shape, lo, hi):
        return jax.random.uniform(next(ks), shape, F32, lo, hi)

    def lru_lambda():
        a_pow = unif((LRU_WIDTH,), 0.9, 0.999)
        a_base = a_pow ** (1.0 / LRU_C)
        return jnp.log(a_base) - jnp.log1p(-a_base)

    return {
        'x': nrm((BATCH, SEQ, d), 1.0),
        'meta_tokens': nrm((N_META, d), 1.0),
        'mix_norm_0': gain(d),
        'w_in_0': nrm((d, sum(EVEN_SIZES)), d ** -0.5),
        'conv_w_0': nrm((CONV_WIDTH, LRU_WIDTH), CONV_WIDTH ** -0.5),
        'conv_b_0': nrm((LRU_WIDTH,), 0.01),
        'lru_wa_0': nrm((LRU_BLOCKS, LRU_BLOCK_DIM, LRU_BLOCK_DIM), LRU_BLOCK_DIM ** -0.5),
        'lru_ba_0': nrm((LRU_BLOCKS, LRU_BLOCK_DIM), 0.01),
        'lru_wx_0': nrm((LRU_BLOCKS, LRU_BLOCK_DIM, LRU_BLOCK_DIM), LRU_BLOCK_DIM ** -0.5),
        'lru_bx_0': nrm((LRU_BLOCKS, LRU_BLOCK_DIM), 0.01),
        'lru_lam_0': lru_lambda(),
        'diff_q_gain_0': gain(DIFF_HEAD_DIM),
        'diff_k_gain_0': gain(DIFF_HEAD_DIM),
        'diff_lq1_0': nrm((DIFF_HEAD_DIM,), 0.1),
        'diff_lk1_0': nrm((DIFF_HEAD_DIM,), 0.1),
        'diff_lq2_0': nrm((DIFF_HEAD_DIM,), 0.1),
        'diff_lk2_0': nrm((DIFF_HEAD_DIM,), 0.1),
        'diff_sub_gain_0': gain(2 * DIFF_HEAD_DIM),
        'w_out_0': nrm((2 * GROUP_WIDTH, d), (2 * GROUP_WIDTH) ** -0.5),
        'ffn_norm_0': gain(d),
        'ffn_up_0': nrm((d, D_FF), d ** -0.5),
        'ffn_down_0': nrm((D_FF, d), D_FF ** -0.5),
        'mix_norm_1': gain(d),
        'w_in_1': nrm((d, sum(ODD_SIZES)), d ** -0.5),
        'rwkv_mu_1': unif((RWKV_SLAB,), 0.0, 1.0),
        'rwkv_w0_1': unif((GROUP_WIDTH,), -6.0, -1.0),
        'rwkv_w_up_1': nrm((DECAY_LORA, GROUP_WIDTH), 0.5 * DECAY_LORA ** -0.5),
        'rwkv_a0_1': nrm((GROUP_WIDTH,), 0.1),
        'rwkv_a_up_1': nrm((AAA_LORA, GROUP_WIDTH), 0.5 * AAA_LORA ** -0.5),
        'rwkv_g_up_1': nrm((GATE_LORA, GROUP_WIDTH), GATE_LORA ** -0.5),
        'rwkv_k_k_1': 0.85 + nrm((GROUP_WIDTH,), 0.02),
        'rwkv_k_a_1': gain(GROUP_WIDTH),
        'rwkv_r_k_1': nrm((RWKV_HEADS, RWKV_HEAD_DIM), 0.1),
        'rwkv_gn_w_1': gain(GROUP_WIDTH),
        'rwkv_gn_b_1': nrm((GROUP_WIDTH,), 0.01),
        'fox_q_gain_1': gain(FOX_HEAD_DIM),
        'fox_k_gain_1': gain(FOX_HEAD_DIM),
        'fox_f_bias_1': unif((FOX_HEADS,), 1.0, 4.0),
        'w_out_1': nrm((2 * GROUP_WIDTH, d), (2 * GROUP_WIDTH) ** -0.5),
        'ffn_norm_1': gain(d),
        'ffn_up_1': nrm((d, D_FF), d ** -0.5),
        'ffn_down_1': nrm((D_FF, d), D_FF ** -0.5),
    }


def reference(x, meta_tokens,
              mix_norm_0, w_in_0, conv_w_0, conv_b_0, lru_wa_0, lru_ba_0, lru_wx_0, lru_bx_0, lru_lam_0,
              diff_q_gain_0, diff_k_gain_0, diff_lq1_0, diff_lk1_0, diff_lq2_0, diff_lk2_0, diff_sub_gain_0,
              w_out_0, ffn_norm_0, ffn_up_0, ffn_down_0,
              mix_norm_1, w_in_1, rwkv_mu_1, rwkv_w0_1, rwkv_w_up_1, rwkv_a0_1, rwkv_a_up_1, rwkv_g_up_1,
              rwkv_k_k_1, rwkv_k_a_1, rwkv_r_k_1, rwkv_gn_w_1, rwkv_gn_b_1,
              fox_q_gain_1, fox_k_gain_1, fox_f_bias_1, w_out_1, ffn_norm_1, ffn_up_1, ffn_down_1):
    B = x.shape[0]
    meta = jnp.broadcast_to(meta_tokens[None].astype(x.dtype), (B, N_META, D_MODEL))
    h = jnp.concatenate([meta, x], axis=1)
    pos = jnp.arange(h.shape[1])

    even_params = (w_in_0, conv_w_0, conv_b_0, lru_wa_0, lru_ba_0, lru_wx_0, lru_bx_0, lru_lam_0,
                   diff_q_gain_0, diff_k_gain_0, diff_lq1_0, diff_lk1_0, diff_lq2_0, diff_lk2_0,
                   diff_sub_gain_0, w_out_0)
    odd_params = (w_in_1, rwkv_mu_1, rwkv_w0_1, rwkv_w_up_1, rwkv_a0_1, rwkv_a_up_1, rwkv_g_up_1,
                  rwkv_k_k_1, rwkv_k_a_1, rwkv_r_k_1, rwkv_gn_w_1, rwkv_gn_b_1,
                  fox_q_gain_1, fox_k_gain_1, fox_f_bias_1, w_out_1)
    mix_norms = (mix_norm_0, mix_norm_1)
    ffn_params = ((ffn_norm_0, ffn_up_0, ffn_down_0), (ffn_norm_1, ffn_up_1, ffn_down_1))

    for layer in range(DEPTH):
        hn = _rms_norm(h, mix_norms[layer])
        if layer % 2 == 0:
            h = h + _even_mixer(hn, pos, _diff_lambda_init(layer), *even_params)
        else:
            h = h + _odd_mixer(hn, *odd_params)
        f_gain, f_up, f_down = ffn_params[layer]
        h = h + _squared_relu_mlp(_rms_norm(h, f_gain), f_up, f_down)
    return h[:, N_META:]
```

```python
import math
import numpy as np
import concourse.bass as bass
import concourse.mybir as mybir
from concourse.bass_utils import run_bass_kernel_spmd

F32 = mybir.dt.float32
BF16 = mybir.dt.bfloat16
AF = mybir.ActivationFunctionType
ALU = mybir.AluOpType
AX = mybir.AxisListType

ENGS = ("pe", "act", "dve", "pool", "sp")
STRICT_SAME_ENGINE = False
SAME_ENGINE_SYNC = {"pe": False, "act": True, "dve": True, "pool": True, "sp": False}


class Tile:
    def __init__(self, k, handle, name):
        self.k = k
        self.h = handle
        self.name = name
        self.last_write = None
        self.reads = []
        self.dsem = None
        self.dcount = 0

    def __getitem__(self, idx):
        return self.h[idx]

    def ap(self):
        return self.h[:] if not hasattr(self.h, "ap") else self.h.ap()


class Op:
    __slots__ = ("eng", "fn", "deps", "signal", "sigval", "is_dma", "dma_tile", "dma_val", "idx")

    def __init__(self, eng, fn):
        self.eng = eng
        self.fn = fn
        self.deps = []
        self.signal = False
        self.sigval = None
        self.is_dma = False
        self.dma_tile = None
        self.dma_val = None


class K:
    def __init__(self):
        self.nc = bass.Bass("TRN2", target_bir_lowering=False)
        self.ops = {e: [] for e in ENGS}
        self.tiles = []
        self._n = 0

    def sb(self, shape, dtype=F32, name=None):
        self._n += 1
        name = name or f"sb{self._n}"
        h = self.nc.alloc_sbuf_tensor(name, list(shape), dtype)
        t = Tile(self, h, name)
        self.tiles.append(t)
        return t

    def ps(self, shape, dtype=F32, name=None):
        self._n += 1
        name = name or f"ps{self._n}"
        h = self.nc.alloc_psum_tensor(name, list(shape), dtype)
        t = Tile(self, h, name)
        self.tiles.append(t)
        return t

    def dram(self, name, shape, dtype=F32, kind="Internal"):
        h = self.nc.dram_tensor(name, list(shape), dtype, kind=kind)
        t = Tile(self, h, name)
        self.tiles.append(t)
        return t

    def _record(self, op, reads, writes):
        deps = []
        for t in reads:
            if t.last_write is not None:
                deps.append((t.last_write, True))
        for t in writes:
            if t.last_write is not None:
                deps.append((t.last_write, False))
            last = {}
            for r in t.reads:
                if r.is_dma:
                    last[("dma", id(r.dma_tile))] = r
                else:
                    last[r.eng] = r
            deps.extend((r, False) for r in last.values())
        for d, raw in deps:
            if d is op:
                continue
            if d.is_dma:
                op.deps.append(("dma", d.dma_tile, d.dma_tile.dcount * 16))
            else:
                if d.eng == op.eng and not ((raw or STRICT_SAME_ENGINE) and SAME_ENGINE_SYNC[op.eng]):
                    continue
                d.signal = True
                op.deps.append(("op", d, None))
        for t in reads:
            key = ("dma", id(op.dma_tile)) if op.is_dma else op.eng
            t.reads = [r for r in t.reads if (("dma", id(r.dma_tile)) if r.is_dma else r.eng) != key]
            t.reads.append(op)
        for t in writes:
            t.last_write = op
            t.reads = []
        self.ops[op.eng].append(op)
        return op

    def op(self, eng, fn, reads=(), writes=()):
        return self._record(Op(eng, fn), list(reads), list(writes))

    def dma(self, eng, out_tile, out_ap, in_tile, in_ap, sem_tile=None, **kw):
        st = sem_tile or out_tile
        op = Op(eng, None)
        op.is_dma = True
        op.dma_tile = st
        self._record(op, [in_tile], [out_tile])
        st.dcount += 1
        op.dma_val = st.dcount * 16
        op.fn = lambda e, o=out_ap, i=in_ap, kw=kw: e.dma_start(out=o, in_=i, **kw)
        return op

    def emit(self, final_waits=()):
        nc = self.nc
        import contextlib
        with contextlib.ExitStack() as st:
            esem = {e: st.enter_context(nc.semaphore(f"s_{e}")) for e in ENGS}
            for t in self.tiles:
                if t.dcount > 0:
                    t.dsem = st.enter_context(nc.semaphore(f"d_{t.name}"))
            for e in ENGS:
                c = 0
                for op in self.ops[e]:
                    if op.signal and not op.is_dma:
                        c += 1
                        op.sigval = c
            self.sigmax = {e: max([op.sigval or 0 for op in self.ops[e]] + [0]) for e in ENGS}
            self.dmax = max([t.dcount * 16 for t in self.tiles] + [0])
            import os
            if os.environ.get('FW_VERBOSE'):
                print('sigmax', self.sigmax, 'dma max', self.dmax, 'nops', {e: len(self.ops[e]) for e in ENGS}, flush=True)
            block = st.enter_context(nc.Block())
            engobj = {}

            def run(e, eng):
                known = {}
                for op in self.ops[e]:
                    for d in op.deps:
                        if d[0] == "dma":
                            key, val, sem = ("d", id(d[1])), d[2], d[1].dsem
                        else:
                            key, val, sem = ("e", d[1].eng), d[1].sigval, esem[d[1].eng]
                        if known.get(key, 0) >= val:
                            continue
                        known[key] = val
                        eng.wait_ge(sem, val)
                    ins = op.fn(eng)
                    if op.is_dma:
                        ins.then_inc(op.dma_tile.dsem, 16)
                    elif op.signal:
                        ins.then_inc(esem[e], 1)
                for (t, ) in [(x,) for x in final_waits]:
                    pass

            def mk(e):
                def f(eng):
                    run(e, eng)
                    if e == "sp":
                        for t in self.tiles:
                            if t.dcount > 0:
                                eng.wait_ge(t.dsem, t.dcount * 16)
                return f

            block.tensor(mk("pe"))
            block.scalar(mk("act"))
            block.vector(mk("dve"))
            block.gpsimd(mk("pool"))
            block.sync(mk("sp"))
        return nc


def _act(self, out, in_, func, reads, writes, **kw):
    return self.op("act", lambda e: e.activation(out=out, in_=in_, func=func, **kw), reads, writes)


def _tt(self, eng, out, in0, in1, op, reads, writes):
    return self.op(eng, lambda e: e.tensor_tensor(out=out, in0=in0, in1=in1, op=op), reads, writes)


def _ts(self, eng, out, in0, s1, s2, op0, op1, reads, writes):
    if op1 is None:
        return self.op(eng, lambda e: e.tensor_scalar(out=out, in0=in0, scalar1=s1, scalar2=None, op0=op0), reads, writes)
    return self.op(eng, lambda e: e.tensor_scalar(out=out, in0=in0, scalar1=s1, scalar2=s2, op0=op0, op1=op1), reads, writes)


def _stt(self, out, in0, scalar, in1, op0, op1, reads, writes):
    return self.op("dve", lambda e: e.scalar_tensor_tensor(out=out, in0=in0, scalar=scalar, in1=in1, op0=op0, op1=op1), reads, writes)


def _mm(self, out, lhsT, rhs, start, stop, reads, writes, **kw):
    return self.op("pe", lambda e: e.matmul(out, lhsT=lhsT, rhs=rhs, start=start, stop=stop, **kw), reads, writes)


def _copy(self, eng, out, in_, reads, writes):
    if eng == "act":
        return self.op("act", lambda e: e.activation(out=out, in_=in_, func=AF.Copy), reads, writes)
    return self.op(eng, lambda e: e.tensor_copy(out=out, in_=in_), reads, writes)


def _recip(self, out, in_, reads, writes):
    return self.op("dve", lambda e: e.reciprocal(out=out, in_=in_), reads, writes)


def _memset(self, eng, t, ap, val):
    return self.op(eng, lambda e: e.memset(ap, val), [], [t])


K.act = _act
K.tt = _tt
K.ts = _ts
K.stt = _stt
K.mm = _mm
K.copy = _copy
K.recip = _recip
K.memset = _memset


NORM_EPS = 1e-6


def build_gemm(mode, D=4096, NT=2052, SUB=342, NSUB=2, DFF=16384, NIN=10240, HB=4, WG=2):
    k = K()
    KC = D // 128
    DCT = min(512, D)
    ST = SUB * NSUB
    assert NT % ST == 0
    n_super = NT // ST
    hT = k.dram("hT", [D, NT], F32, kind="ExternalInput")
    if mode in (3, 5):
        yT = k.dram("yT", [D, NT], F32, kind="ExternalInput")
        w_out = k.dram("w_out", [D, D], F32, kind="ExternalInput")
        ffn_g = k.dram("ffn_g", [128, KC], F32, kind="ExternalInput")
        w_up = k.dram("w_up", [D, DFF], F32, kind="ExternalInput")
        w_down = k.dram("w_down", [DFF, D], F32, kind="ExternalInput")
        hout = k.dram("hout", [D, NT], F32, kind="ExternalOutput")
    if mode in (1, 3):
        mix_g = k.dram("mix_g", [128, KC], F32, kind="ExternalInput")
        w_in = k.dram("w_in", [D, NIN], F32, kind="ExternalInput")
        UT = k.dram("UT", [NIN, NT], F32, kind="ExternalOutput")

    hbuf = k.sb([128, KC, ST], F32, "hbuf")
    hv = [Tile(k, hbuf.h, f"hv{i}") for i in range(KC)]
    hsem = hbuf
    hn = k.sb([128, KC, ST], BF16, "hn")
    hnv = [Tile(k, hn.h, f"hnv{i}") for i in range(KC)]
    k.tiles.extend(hv + hnv)
    WELEMS = max(KC * WG * 128, HB * 512)
    NWB = 3
    wbufs = [k.sb([128, WELEMS], BF16, f"wb{i}") for i in range(NWB)]
    ones = k.sb([128, 128], BF16, "ones")
    k.op("pool", lambda e: e.memset(ones[:], 1.0), writes=[ones])
    gains = {}
    if mode in (3, 5):
        gains["ffn"] = k.sb([128, KC], F32, "g_ffn")
        k.dma("sp", gains["ffn"], gains["ffn"][:], ffn_g, ffn_g.ap())
        Hb = [k.sb([128, HB, ST], BF16, f"Hb{i}") for i in range(2)]
    if mode in (1, 3):
        gains["mix"] = k.sb([128, KC], F32, "g_mix")
        k.dma("sp", gains["mix"], gains["mix"][:], mix_g, mix_g.ap())
        ostage = [k.sb([128, ST], F32, f"ost{i}") for i in range(2)]
    sqb = [k.sb([128, SUB], BF16, f"sq{i}") for i in range(4)]
    rstd = [k.sb([128, SUB], F32, f"rstd{i}") for i in range(NSUB)]
    tmpf = [k.sb([128, SUB], F32, f"tmpf{i}") for i in range(4)]
    epsb = k.sb([128, 1], F32, "epsb")
    k.op("pool", lambda e: e.memset(epsb[:], NORM_EPS), writes=[epsb])
    pacc = [k.ps([128, 512], F32, f"pacc{i}") for i in range(6)]
    pstat = [k.ps([128, 512], F32, f"pstat{i}") for i in range(2)]
    st = {"w": 0, "p": 0, "sq": 0, "tf": 0, "os": 0}

    wq = []

    class WStream:
        def __init__(self):
            self.reqs = []
            self.issued = 0
            self.consumed = 0

        def add(self, wt, r0, nk, c0, ncols):
            self.reqs.append((wt, r0, nk, c0, ncols))

        def issue_upto(self, n):
            while self.issued < min(n, len(self.reqs)):
                wt, r0, nk, c0, ncols = self.reqs[self.issued]
                buf = wbufs[self.issued % NWB]
                src = wt.ap()[r0:r0 + nk * 128, c0:c0 + ncols].rearrange("(kc p) n -> p kc n", p=128)
                dst = buf[:, 0:nk * ncols].rearrange("p (kc n) -> p kc n", kc=nk)
                k.dma("pool", buf, dst, wt, src)
                self.issued += 1

        def get(self):
            i = self.consumed
            self.issue_upto(i + NWB)
            self.consumed += 1
            wt, r0, nk, c0, ncols = self.reqs[i]
            buf = wbufs[i % NWB]
            return buf, buf[:, 0:nk * ncols].rearrange("p (kc n) -> p kc n", kc=nk)

    ws = WStream()

    def plan_linear(wt, Kdim, Ndim, r0=0):
        nk = Kdim // 128
        c = 0
        while c < Ndim:
            nc_ = min(WG * 128, Ndim - c)
            ws.add(wt, r0, nk, c, nc_)
            c += nc_

    for s in range(n_super):
        if mode in (3, 5):
            plan_linear(w_out, D, D)
            for hb in range(DFF // (HB * 128)):
                for c in range(0, HB * 128, WG * 128):
                    ws.add(w_up, 0, KC, hb * HB * 128 + c, WG * 128)
                for c in range(0, D, DCT):
                    ws.add(w_down, hb * HB * 128, HB, c, DCT)
        if mode in (1, 3):
            plan_linear(w_in, D, NIN)

    def nxt(key, n):
        v = st[key]
        st[key] = (v + 1) % n
        return v

    def linear_chunks(in_views, in_buf, nk, ncols_total, evac, col_tile):
        c0 = 0
        ci = 0
        while c0 < ncols_total:
            ncols = min(col_tile, ncols_total - c0)
            wbuf, wap = ws.get()
            for j0 in range(0, ncols, 128):
                m = min(128, ncols - j0)
                pts = [pacc[(st["p"] + s_) % 6] for s_ in range(NSUB)]
                st["p"] = (st["p"] + NSUB) % 6
                for kc in range(nk):
                    for s_ in range(NSUB):
                        pt = pts[s_]
                        k.op("pe", lambda e, pt=pt, kc=kc, j0=j0, m=m, s_=s_, wap=wap:
                             e.matmul(pt[0:m, 0:SUB], lhsT=wap[:, kc, j0:j0 + m],
                                      rhs=in_buf[:, kc, s_ * SUB:(s_ + 1) * SUB],
                                      start=(kc == 0), stop=(kc == nk - 1)),
                             reads=[wbuf, in_views[kc]], writes=[pt])
                evac(ci, m, pts)
                ci += 1
            c0 += ncols

    def rmsnorm(gain_tile):
        for s_ in range(NSUB):
            ps_ = pstat[s_ % 2]
            for kc in range(KC):
                sq = sqb[nxt("sq", 4)]
                k.op("act", lambda e, sq=sq, kc=kc, s_=s_: e.activation(
                    out=sq[:], in_=hbuf[:, kc, s_ * SUB:(s_ + 1) * SUB], func=AF.Square),
                    reads=[hv[kc]], writes=[sq])
                k.op("pe", lambda e, sq=sq, kc=kc, ps_=ps_: e.matmul(
                    ps_[:, 0:SUB], lhsT=ones[:], rhs=sq[:], start=(kc == 0), stop=(kc == KC - 1)),
                    reads=[sq, ones], writes=[ps_])
            r = rstd[s_]
            k.op("act", lambda e, r=r, ps_=ps_: e.activation(
                out=r[:], in_=ps_[:, 0:SUB], func=AF.Sqrt, scale=1.0 / D, bias=epsb[:]),
                reads=[ps_, epsb], writes=[r])
            k.op("dve", lambda e, r=r: e.reciprocal(out=r[:], in_=r[:]), reads=[r], writes=[r])
            for kc in range(KC):
                k.op("dve", lambda e, kc=kc, s_=s_, r=r: e.scalar_tensor_tensor(
                    out=hn[:, kc, s_ * SUB:(s_ + 1) * SUB], in0=hbuf[:, kc, s_ * SUB:(s_ + 1) * SUB],
                    scalar=gain_tile[:, kc:kc + 1], in1=r[:], op0=ALU.mult, op1=ALU.mult),
                    reads=[hv[kc], r, gain_tile], writes=[hnv[kc]])

    for s in range(n_super):
        t0 = s * ST
        for kc in range(KC):
            k.dma("sp", hv[kc], hbuf[:, kc, :], hT, hT.ap()[kc * 128:(kc + 1) * 128, t0:t0 + ST], sem_tile=hsem)
        if mode in (3, 5):
            for kc in range(KC):
                k.dma("pool", hnv[kc], hn[:, kc, :], yT, yT.ap()[kc * 128:(kc + 1) * 128, t0:t0 + ST], sem_tile=hn)

            def evac_add(ci, m, pts):
                for s_ in range(NSUB):
                    k.op("dve", lambda e, ci=ci, s_=s_, pt=pts[s_]: e.tensor_tensor(
                        out=hbuf[:, ci, s_ * SUB:(s_ + 1) * SUB], in0=hbuf[:, ci, s_ * SUB:(s_ + 1) * SUB],
                        in1=pt[:, 0:SUB], op=ALU.add), reads=[hv[ci], pts[s_]], writes=[hv[ci]])
            linear_chunks(hnv, hn, KC, D, evac_add, WG * 128)
            rmsnorm(gains["ffn"])
            for hb in range(DFF // (HB * 128)):
                H = Hb[hb % 2]

                def evac_relu2(ci, m, pts, H=H):
                    for s_ in range(NSUB):
                        tf = tmpf[nxt("tf", 4)]
                        k.op("act", lambda e, tf=tf, pt=pts[s_]: e.activation(
                            out=tf[:], in_=pt[:, 0:SUB], func=AF.Relu), reads=[pts[s_]], writes=[tf])
                        k.op("pool", lambda e, tf=tf, ci=ci, s_=s_, H=H: e.tensor_tensor(
                            out=H[:, ci, s_ * SUB:(s_ + 1) * SUB], in0=tf[:], in1=tf[:], op=ALU.mult),
                            reads=[tf], writes=[H])
                linear_chunks(hnv, hn, KC, HB * 128, evac_relu2, WG * 128)
                Hviews = [H] * HB
                linear_chunks(Hviews, H, HB, D, evac_add, DCT)
            if mode == 3:
                for kc in range(KC):
                    k.dma("sp", hout, hout.ap()[kc * 128:(kc + 1) * 128, t0:t0 + ST], hv[kc], hbuf[:, kc, :], sem_tile=hsem)
        if mode in (1, 3):
            rmsnorm(gains["mix"])

            def evac_out(ci, m, pts):
                o = ostage[nxt("os", 2)]
                for s_ in range(NSUB):
                    eng = "act" if (s_ + ci) % 2 == 0 else "dve"
                    if eng == "act":
                        k.op("act", lambda e, o=o, s_=s_, pt=pts[s_], m=m: e.activation(
                            out=o[0:m, s_ * SUB:(s_ + 1) * SUB], in_=pt[0:m, 0:SUB], func=AF.Copy),
                            reads=[pts[s_]], writes=[o])
                    else:
                        k.op("dve", lambda e, o=o, s_=s_, pt=pts[s_], m=m: e.tensor_copy(
                            out=o[0:m, s_ * SUB:(s_ + 1) * SUB], in_=pt[0:m, 0:SUB]),
                            reads=[pts[s_]], writes=[o])
                k.dma("sp", UT, UT.ap()[ci * 128:ci * 128 + m, t0:t0 + ST], o, o[0:m, :], sem_tile=o)
            linear_chunks(hnv, hn, KC, NIN, evac_out, WG * 128)
        if mode == 5:
            for kc in range(KC):
                k.dma("sp", hout, hout.ap()[kc * 128:(kc + 1) * 128, t0:t0 + ST], hv[kc], hbuf[:, kc, :], sem_tile=hsem)
    assert ws.consumed == len(ws.reqs), (ws.consumed, len(ws.reqs))
    return k.emit()


PAD = 112
QK_EPS = 1e-6


def build_mix0(T_real=8208, NB=4, NH=2, LW=456, lambda_init=0.2):
    k = K()
    TP = PAD + T_real
    assert TP % 128 == 0 and (TP - 128) % 512 == 0 and T_real % LW == 0
    NKB = TP // 128
    CH = NB * 128
    xbT = k.dram("xbT", [CH, TP], F32, kind="ExternalInput")
    gbT = k.dram("gbT", [CH, TP], F32, kind="ExternalInput")
    lru_p = k.dram("lru_p", [128, NB, 8], F32, kind="ExternalInput")
    lru_w = k.dram("lru_w", [128, NB, 2, 128], F32, kind="ExternalInput")
    qT = k.dram("qT", [NH * 256, TP], F32, kind="ExternalInput")
    kT = k.dram("kT", [NH * 256, TP], F32, kind="ExternalInput")
    vtm = k.dram("vtm", [128, NKB, NH * 256], F32, kind="ExternalInput")
    att_p = k.dram("att_p", [128, 8], F32, kind="ExternalInput")
    rope = k.dram("rope", [32, 2, TP], F32, kind="ExternalInput")
    cmask = k.dram("cmask", [128, 4, 512], F32, kind="ExternalInput")
    rmat = k.dram("rmat", [128, 32], F32, kind="ExternalInput")
    valid = k.dram("valid", [128, NKB], F32, kind="ExternalInput")
    yT = k.dram("yT", [CH + NH * 256, T_real], F32, kind="ExternalOutput")

    ones = k.sb([128, 128], BF16, "ones")
    k.memset("pool", ones, ones[:], 1.0)
    epsb = k.sb([128, 1], F32, "epsb")
    k.memset("pool", epsb, epsb[:], QK_EPS)
    tiny = k.sb([128, 1], F32, "tiny")
    k.memset("pool", tiny, tiny[:], 1e-30)

    P = [k.ps([128, 512], F32, f"P{i}") for i in range(8)]

    lp = k.sb([128, NB, 8], F32, "lp")
    k.dma("sp", lp, lp[:], lru_p, lru_p.ap())
    lw = k.sb([128, NB, 2, 128], BF16, "lw")
    k.dma("pool", lw, lw[:], lru_w, lru_w.ap())
    nsp8 = k.sb([128, NB], F32, "nsp8")
    k.act(nsp8[:], lp[:, :, 7], AF.Exp, [lp], [nsp8], scale=-1.0)
    oneb = k.sb([128, 1], F32, "oneb")
    k.memset("pool", oneb, oneb[:], 1.0)
    k.act(nsp8[:], nsp8[:], AF.Ln, [nsp8, oneb], [nsp8], bias=oneb[:])
    k.ts("dve", nsp8[:], nsp8[:], -8.0, None, ALU.mult, None, [nsp8], [nsp8])

    NLT = T_real // LW
    xin = [k.sb([128, LW + 3], F32, f"xin{i}") for i in range(2)]
    gin = [k.sb([128, LW], F32, f"gin{i}") for i in range(2)]
    xc = [k.sb([128, LW], F32, f"xc{i}") for i in range(2)]
    xcb = [k.sb([128, LW], BF16, f"xcb{i}") for i in range(2)]
    rr = [k.sb([128, LW], F32, f"rr{i}") for i in range(2)]
    ii = [k.sb([128, LW], F32, f"ii{i}") for i in range(2)]
    aa = [k.sb([128, LW], F32, f"aa{i}") for i in range(2)]
    mm_ = [k.sb([128, LW], F32, f"mm{i}") for i in range(2)]
    g1 = [k.sb([128, LW], F32, f"g1{i}") for i in range(2)]
    hh = [k.sb([128, LW], F32, f"hh{i}") for i in range(2)]
    hlast = [k.sb([128, 1], F32, f"hlast{n}") for n in range(NB)]
    oo = [k.sb([128, LW], F32, f"oo{i}") for i in range(2)]
    it = 0
    for lt in range(NLT):
        t0 = PAD + lt * LW
        for n in range(NB):
            b = it % 2
            it += 1
            X, G, XC, XB, R, I, A, M, G1, O = xin[b], gin[b], xc[b], xcb[b], rr[b], ii[b], aa[b], mm_[b], g1[b], oo[b]
            H = hh[b]
            Hprev = hlast[n]
            k.dma("sp", X, X[:], xbT, xbT.ap()[n * 128:(n + 1) * 128, t0 - 3:t0 + LW])
            k.dma("sp", G, G[:], gbT, gbT.ap()[n * 128:(n + 1) * 128, t0:t0 + LW])
            k.ts("dve", XC[:], X[:, 0:LW], lp[:, n, 0:1], lp[:, n, 4:5], ALU.mult, ALU.add, [X, lp], [XC])
            for j in (1, 2, 3):
                k.stt(XC[:], X[:, j:j + LW], lp[:, n, j:j + 1], XC[:], ALU.mult, ALU.add, [X, lp, XC], [XC])
            k.copy("act", XB[:], XC[:], [XC], [XB])
            pr, pi = P[0 + 2 * b], P[1 + 2 * b]
            k.mm(pr[:, 0:LW], lw[:, n, 0, :], XB[:], True, True, [lw, XB], [pr])
            k.mm(pi[:, 0:LW], lw[:, n, 1, :], XB[:], True, True, [lw, XB], [pi])
            k.act(R[:], pr[:, 0:LW], AF.Sigmoid, [pr, lp], [R], bias=lp[:, n, 5:6])
            k.act(I[:], pi[:, 0:LW], AF.Sigmoid, [pi, lp], [I], bias=lp[:, n, 6:7])
            k.act(A[:], R[:], AF.Exp, [R, nsp8], [A], scale=nsp8[:, n:n + 1])
            k.tt("dve", M[:], A[:], A[:], ALU.mult, [A], [M])
            k.ts("dve", M[:], M[:], -1.0, 1.0, ALU.mult, ALU.add, [M], [M])
            k.act(M[:], M[:], AF.Sqrt, [M], [M])
            k.tt("dve", I[:], I[:], XC[:], ALU.mult, [I, XC], [I])
            k.tt("dve", I[:], I[:], M[:], ALU.mult, [I, M], [I])
            init = 0.0 if lt == 0 else Hprev[:, 0:1]
            k.op("dve", lambda e, H=H, A=A, I=I, init=init: e.tensor_tensor_scan(
                out=H[:], data0=A[:], data1=I[:], initial=init, op0=ALU.mult, op1=ALU.add),
                [A, I] + ([Hprev] if lt else []), [H])
            k.copy("act", Hprev[:, 0:1], H[:, LW - 1:LW], [H], [Hprev])
            k.tt("pool", G1[:], G[:], G[:], ALU.mult, [G], [G1])
            k.ts("pool", G1[:], G1[:], 0.044715, 1.0, ALU.mult, ALU.add, [G1], [G1])
            k.tt("pool", G1[:], G1[:], G[:], ALU.mult, [G1, G], [G1])
            k.act(G1[:], G1[:], AF.Sigmoid, [G1], [G1], scale=1.5957691216057308)
            k.tt("pool", G1[:], G1[:], G[:], ALU.mult, [G1, G], [G1])
            k.tt("dve", O[:], H[:], G1[:], ALU.mult, [H, G1], [O])
            k.dma("sp", yT, yT.ap()[n * 128:(n + 1) * 128, lt * LW:(lt + 1) * LW], O, O[:], sem_tile=O)

    ap_ = k.sb([128, 8], F32, "attp")
    k.dma("sp", ap_, ap_[:], att_p, att_p.ap())
    rm = k.sb([128, 32], BF16, "rm")
    k.dma("pool", rm, rm[:], rmat, rmat.ap())
    cm = k.sb([128, 4, 512], BF16, "cm")
    k.dma("pool", cm, cm[:], cmask, cmask.ap())
    vl = k.sb([128, NKB], F32, "vl")
    k.dma("sp", vl, vl[:], valid, valid.ap())
    vones = k.sb([128, 128], BF16, "vones")
    k.ts("pool", vones[:], ones[:], vl[:, 0:1], None, ALU.mult, None, [ones, vl], [vones])
    lprod = k.sb([128, 2], BF16, "lprod")
    lprodf = k.sb([128, 2], F32, "lprodf")
    k.tt("dve", lprodf[:, 0:1], ap_[:, 2:3], ap_[:, 3:4], ALU.mult, [ap_], [lprodf])
    k.tt("dve", lprodf[:, 1:2], ap_[:, 4:5], ap_[:, 5:6], ALU.mult, [ap_], [lprodf])
    k.copy("dve", lprod[:], lprodf[:], [lprodf], [lprod])
    k.mm(P[7][:, 0:2], ones[:], lprod[:], True, True, [ones, lprod], [P[7]])
    lam = k.sb([128, 2], F32, "lam")
    k.act(lam[:], P[7][:, 0:2], AF.Exp, [P[7]], [lam])
    nlam = k.sb([128, 1], F32, "nlam")
    k.tt("dve", nlam[:], lam[:, 1:2], lam[:, 0:1], ALU.subtract, [lam], [nlam])
    k.ts("dve", nlam[:], nlam[:], -lambda_init, None, ALU.add, None, [nlam], [nlam])
    sg = k.sb([128, 2], F32, "sg")
    k.ts("dve", sg[:], ap_[:, 6:8], 1.0 - lambda_init, None, ALU.mult, None, [ap_], [sg])

    qk = [k.sb([128, TP], BF16, f"qk{i}") for i in range(4)]
    vb = k.sb([128, NKB, 256], BF16, "vb")
    xt = [k.sb([128, 512], F32, f"xt{i}") for i in range(2)]
    sqt = [k.sb([128, 512], BF16, f"sqt{i}") for i in range(2)]
    rs = [k.sb([128, 512], F32, f"rs{i}") for i in range(2)]
    xn = [k.sb([128, 512], F32, f"xn{i}") for i in range(2)]
    xnb = [k.sb([128, 512], BF16, f"xnb{i}") for i in range(2)]
    cs = [k.sb([32, 2, 512], F32, f"cs{i}") for i in range(2)]
    t1 = [k.sb([32, 512], F32, f"t1{i}") for i in range(2)]
    t2 = [k.sb([32, 512], F32, f"t2{i}") for i in range(2)]
    pt = [[k.sb([128, 512], BF16, f"pt{s}_{i}") for i in range(2)] for s in range(2)]
    rz = [k.sb([128, 512], F32, f"rz{i}") for i in range(2)]
    ob = [k.sb([128, 512], F32, f"ob{i}") for i in range(2)]
    o2 = k.sb([128, 512], F32, "o2")
    osq = [k.sb([128, 512], BF16, f"osq{i}") for i in range(2)]
    yo = [k.sb([128, 512], F32, f"yo{i}") for i in range(2)]
    scale = 128 ** -0.5

    qtiles = [(0, 128)] + [(128 + 512 * i, 512) for i in range((TP - 128) // 512)]
    for h in range(NH):
        k.dma("pool", vb, vb[:], vtm, vtm.ap()[:, :, h * 256:(h + 1) * 256])
        pi_ = 0
        for ti, (src, gcol) in enumerate([(qT, 0), (qT, 0), (kT, 1), (kT, 1)]):
            sub = ti % 2
            row0 = h * 256 + sub * 128
            dst = qk[ti]
            for (c0, w) in qtiles:
                b = pi_ % 2
                pi_ += 1
                X, SQ, RS, XN, XNB, CS, T1, T2 = xt[b], sqt[b], rs[b], xn[b], xnb[b], cs[b], t1[b], t2[b]
                ps, pr = P[6], P[7]
                k.dma("sp", X, X[:, 0:w], src, src.ap()[row0:row0 + 128, c0:c0 + w])
                k.dma("sp", CS, CS[:, :, 0:w], rope, rope.ap()[:, :, c0:c0 + w])
                k.act(SQ[:, 0:w], X[:, 0:w], AF.Square, [X], [SQ])
                k.mm(ps[:, 0:w], ones[:], SQ[:, 0:w], True, True, [ones, SQ], [ps])
                k.act(RS[:, 0:w], ps[:, 0:w], AF.Sqrt, [ps, epsb], [RS], scale=1.0 / 128, bias=epsb[:])
                k.recip(RS[:, 0:w], RS[:, 0:w], [RS], [RS])
                k.stt(XN[:, 0:w], X[:, 0:w], ap_[:, gcol:gcol + 1], RS[:, 0:w], ALU.mult, ALU.mult, [X, ap_, RS], [XN])
                k.copy("act", XNB[:, 0:w], XN[:, 0:w], [XN], [XNB])
                k.mm(pr[0:32, 0:w], rm[:], XNB[:, 0:w], True, True, [rm, XNB], [pr])
                k.tt("dve", T1[:, 0:w], XN[0:32, 0:w], CS[:, 0, 0:w], ALU.mult, [XN, CS], [T1])
                k.tt("dve", T2[:, 0:w], pr[0:32, 0:w], CS[:, 1, 0:w], ALU.mult, [pr, CS], [T2])
                k.copy("act", dst[:, c0:c0 + w], XN[:, 0:w], [XN], [dst])
                k.tt("dve", dst[0:32, c0:c0 + w], T1[:, 0:w], T2[:, 0:w], ALU.add, [T1, T2], [dst])
        q1, q2, k1, k2 = qk
        for qi, (c0, w) in enumerate(qtiles):
            nkb = (c0 + w) // 128
            O = [[P[0], P[1]], [P[3], P[4]]]
            Z = [P[2], P[5]]
            S = [P[6], P[7]]
            prev = None
            for kb in range(nkb + 1):
                cur = None
                if kb < nkb:
                    diag = kb * 128 >= c0
                    cur = []
                    for s_, (qq, kk_) in enumerate([(q1, k1), (q2, k2)]):
                        k.mm(S[s_][:, 0:w], kk_[:, kb * 128:(kb + 1) * 128], qq[:, c0:c0 + w], True, True, [kk_, qq], [S[s_]])
                        PT = pt[s_][kb % 2]
                        k.act(PT[:, 0:w], S[s_][:, 0:w], AF.Exp, [S[s_]], [PT], scale=scale)
                        if diag:
                            j = (kb * 128 - c0) // 128
                            k.tt("dve", PT[:, 0:w], PT[:, 0:w], cm[:, j, 0:w], ALU.mult, [PT, cm], [PT])
                        cur.append(PT)
                if prev is not None:
                    pkb = kb - 1
                    for s_ in range(2):
                        PT = prev[s_]
                        first, last = (pkb == 0), (pkb == nkb - 1)
                        for c in range(2):
                            k.mm(O[s_][c][:, 0:w], vb[:, pkb, c * 128:(c + 1) * 128], PT[:, 0:w], first, last, [vb, PT], [O[s_][c]])
                        k.mm(Z[s_][:, 0:w], (vones if pkb == 0 else ones)[:], PT[:, 0:w], first, last, [vones, ones, PT], [Z[s_]])
                prev = cur
            for s_ in range(2):
                k.ts("dve", rz[s_][:, 0:w], Z[s_][:, 0:w], tiny[:, 0:1], None, ALU.add, None, [Z[s_], tiny], [rz[s_]])
                k.recip(rz[s_][:, 0:w], rz[s_][:, 0:w], [rz[s_]], [rz[s_]])
            k.ts("dve", rz[1][:, 0:w], rz[1][:, 0:w], nlam[:, 0:1], None, ALU.mult, None, [rz[1], nlam], [rz[1]])
            pn = S[0]
            for c in range(2):
                k.tt("dve", ob[c][:, 0:w], O[0][c][:, 0:w], rz[0][:, 0:w], ALU.mult, [O[0][c], rz[0]], [ob[c]])
                k.tt("dve", o2[:, 0:w], O[1][c][:, 0:w], rz[1][:, 0:w], ALU.mult, [O[1][c], rz[1]], [o2])
                k.tt("dve", ob[c][:, 0:w], ob[c][:, 0:w], o2[:, 0:w], ALU.add, [ob[c], o2], [ob[c]])
                k.act(osq[c][:, 0:w], ob[c][:, 0:w], AF.Square, [ob[c]], [osq[c]])
                k.mm(pn[:, 0:w], ones[:], osq[c][:, 0:w], c == 0, c == 1, [ones, osq[c]], [pn])
            k.act(rs[0][:, 0:w], pn[:, 0:w], AF.Sqrt, [pn, epsb], [rs[0]], scale=1.0 / 256, bias=epsb[:])
            k.recip(rs[0][:, 0:w], rs[0][:, 0:w], [rs[0]], [rs[0]])
            lo = max(c0, PAD)
            if lo >= c0 + w:
                continue
            off = lo - c0
            for c in range(2):
                k.stt(yo[c][:, 0:w], ob[c][:, 0:w], sg[:, c:c + 1], rs[0][:, 0:w], ALU.mult, ALU.mult, [ob[c], sg, rs[0]], [yo[c]])
                r0 = CH + h * 256 + c * 128
                k.dma("sp", yT, yT.ap()[r0:r0 + 128, lo - PAD:c0 + w - PAD], yo[c], yo[c][:, off:w], sem_tile=yo[c])
    return k.emit()


PAD = 112
QK_EPS = 1e-6


def build_fox(T_real=8208, NH=4, dbg=False):
    k = K()
    TP = PAD + T_real
    assert TP % 128 == 0 and (TP - 128) % 512 == 0
    NKB = TP // 128
    qT = k.dram("qT", [NH * 128, TP], F32, kind="ExternalInput")
    kT = k.dram("kT", [NH * 128, TP], F32, kind="ExternalInput")
    vtm = k.dram("vtm", [128, NKB, NH * 128], F32, kind="ExternalInput")
    fT = k.dram("fT", [NH, TP], F32, kind="ExternalInput")
    fox_p = k.dram("fox_p", [128, 4], F32, kind="ExternalInput")
    cmask = k.dram("cmask", [128, 4, 512], F32, kind="ExternalInput")
    valid = k.dram("valid", [128, NKB], F32, kind="ExternalInput")
    ident = k.dram("ident", [128, 128], F32, kind="ExternalInput")
    yT = k.dram("yT", [NH * 128, T_real], F32, kind="ExternalOutput")

    ones = k.sb([128, 128], BF16, "ones")
    k.memset("pool", ones, ones[:], 1.0)
    epsb = k.sb([128, 1], F32, "epsb")
    k.memset("pool", epsb, epsb[:], QK_EPS)
    tiny = k.sb([128, 1], F32, "tiny")
    k.memset("pool", tiny, tiny[:], 1e-30)
    P = [k.ps([128, 512], F32, f"P{i}") for i in range(8)]

    fp = k.sb([128, 4], F32, "fp")
    k.dma("sp", fp, fp[:], fox_p, fox_p.ap())
    qgs = k.sb([128, 1], F32, "qgs")
    k.ts("dve", qgs[:], fp[:, 0:1], 128 ** -0.5, None, ALU.mult, None, [fp], [qgs])
    cm = k.sb([128, 4, 512], BF16, "cm")
    k.dma("pool", cm, cm[:], cmask, cmask.ap())
    vl = k.sb([128, NKB], F32, "vl")
    k.dma("sp", vl, vl[:], valid, valid.ap())
    vones = k.sb([128, 128], BF16, "vones")
    k.ts("pool", vones[:], ones[:], vl[:, 0:1], None, ALU.mult, None, [ones, vl], [vones])
    idb = k.sb([128, 128], BF16, "idb")
    k.dma("pool", idb, idb[:], ident, ident.ap())

    NPC = 4
    PW = TP // NPC
    cf = k.sb([NH, PW], F32, "cf")
    cc = k.sb([NH, PW], F32, "cc")
    onesrow = k.sb([NH, PW], F32, "onesrow")
    k.memset("pool", onesrow, onesrow[:], 1.0)
    res = k.sb([NH, PW], F32, "res")
    hif = k.sb([NH, PW], F32, "hif")
    carry = k.sb([NH, 1], F32, "carry")
    k.memset("pool", carry, carry[:], 0.0)
    CH = [k.sb([128, TP], BF16, f"CH{i}") for i in range(3)]
    for t in CH:
        k.memset("pool", t, t[:], 0.0)
    for pcs in range(NPC):
        sl = slice(pcs * PW, (pcs + 1) * PW)
        k.dma("sp", cf, cf[:], fT, fT.ap()[:, sl])
        k.act(cf[:], cf[:], AF.Sigmoid, [cf, fp], [cf], bias=fp[0:NH, 2:3])
        k.act(cf[:], cf[:], AF.Ln, [cf], [cf])
        k.op("dve", lambda e: e.tensor_tensor_scan(out=cc[:], data0=onesrow[:], data1=cf[:], initial=carry[:, 0:1],
                                                   op0=ALU.mult, op1=ALU.add), [onesrow, cf, carry], [cc])
        k.copy("act", carry[:, 0:1], cc[:, PW - 1:PW], [cc], [carry])
        k.copy("dve", CH[0][0:NH, sl], cc[:], [cc], [CH[0]])
        k.copy("dve", hif[:], CH[0][0:NH, sl], [CH[0]], [hif])
        k.tt("dve", res[:], cc[:], hif[:], ALU.subtract, [cc, hif], [res])
        k.copy("dve", CH[1][0:NH, sl], res[:], [res], [CH[1]])
        k.copy("dve", hif[:], CH[1][0:NH, sl], [CH[1]], [hif])
        k.tt("dve", res[:], res[:], hif[:], ALU.subtract, [res, hif], [res])
        k.copy("dve", CH[2][0:NH, sl], res[:], [res], [CH[2]])
    nck = k.sb([128, NKB, NH], F32, "nck")
    pc = P[7]
    for kb0 in range(0, NKB, 64):
        nb = min(64, NKB - kb0)
        for kb in range(kb0, kb0 + nb):
            for i in range(3):
                k.mm(pc[:, (kb - kb0) * NH:(kb - kb0 + 1) * NH], CH[i][:, kb * 128:(kb + 1) * 128], idb[:, 0:NH],
                     i == 0, i == 2, [CH[i], idb], [pc])
        k.ts("dve", nck[:, kb0:kb0 + nb, :].rearrange("p a b -> p (a b)"), pc[:, 0:nb * NH], -1.0, None, ALU.mult, None, [pc], [nck])
    aug = k.sb([128, TP], BF16, "aug")
    k.memset("pool", aug, aug[:], 0.0)
    if dbg:
        d1 = k.dram("d_nck", [128, NKB * NH], F32, kind="ExternalOutput")
        k.dma("sp", d1, d1.ap(), nck, nck[:].rearrange("p a b -> p (a b)"), sem_tile=nck)
        d2 = k.dram("d_ch", [8, TP], F32, kind="ExternalOutput")
        dch = k.sb([8, TP], F32, "dch")
        k.copy("dve", dch[:], CH[0][0:8, :], [CH[0]], [dch])
        k.dma("sp", d2, d2.ap(), dch, dch[:], sem_tile=dch)

    qn = k.sb([128, TP], BF16, "qn")
    kn = k.sb([128, TP], BF16, "kn")
    vb = k.sb([128, NKB, 128], BF16, "vb")
    xt = [k.sb([128, 512], F32, f"xt{i}") for i in range(2)]
    sqt = [k.sb([128, 512], BF16, f"sqt{i}") for i in range(2)]
    rs = [k.sb([128, 512], F32, f"rs{i}") for i in range(2)]
    pt = [k.sb([128, 512], BF16, f"pt{i}") for i in range(3)]
    rz = k.sb([128, 512], F32, "rz")
    clampt = k.sb([128, 512], F32, "clampt")
    yo = [k.sb([128, 512], F32, f"yo{i}") for i in range(2)]
    qtiles = [(0, 128)] + [(128 + 512 * i, 512) for i in range((TP - 128) // 512)]
    yi = 0
    for h in range(NH):
        k.dma("pool", vb, vb[:], vtm, vtm.ap()[:, :, h * 128:(h + 1) * 128])
        for i in range(3):
            k.dma("sp", aug, aug[i:i + 1, :], CH[i], CH[i][h:h + 1, :])
        pi_ = 0
        for (src, dst, gain) in [(qT, qn, qgs[:, 0:1]), (kT, kn, fp[:, 1:2])]:
            for (c0, w) in qtiles:
                b = pi_ % 2
                pi_ += 1
                X, SQ, RS = xt[b], sqt[b], rs[b]
                ps = P[6]
                k.dma("sp", X, X[:, 0:w], src, src.ap()[h * 128:(h + 1) * 128, c0:c0 + w])
                k.act(SQ[:, 0:w], X[:, 0:w], AF.Square, [X], [SQ])
                k.mm(ps[:, 0:w], ones[:], SQ[:, 0:w], True, True, [ones, SQ], [ps])
                k.act(RS[:, 0:w], ps[:, 0:w], AF.Sqrt, [ps, epsb], [RS], scale=1.0 / 128, bias=epsb[:])
                k.recip(RS[:, 0:w], RS[:, 0:w], [RS], [RS])
                k.stt(dst[:, c0:c0 + w], X[:, 0:w], gain, RS[:, 0:w], ALU.mult, ALU.mult, [X, fp, qgs, RS], [dst])
        for qi, (c0, w) in enumerate(qtiles):
            nkb = (c0 + w) // 128
            O, Z = P[0], P[1]
            S = [P[2], P[3], P[4]]
            prev = None
            for kb in range(nkb + 1):
                cur = None
                if kb < nkb:
                    diag = kb * 128 >= c0
                    Sb = S[kb % 3]
                    k.mm(Sb[:, 0:w], kn[:, kb * 128:(kb + 1) * 128], qn[:, c0:c0 + w], True, False, [kn, qn], [Sb])
                    k.mm(Sb[:, 0:w], ones[:], aug[:, c0:c0 + w], False, True, [ones, aug], [Sb])
                    PT = pt[kb % 3]
                    if diag:
                        j = (kb * 128 - c0) // 128
                        k.ts("dve", clampt[:, 0:w], Sb[:, 0:w], nck[:, kb, h:h + 1], 30.0, ALU.add, ALU.min, [Sb, nck], [clampt])
                        k.act(PT[:, 0:w], clampt[:, 0:w], AF.Exp, [clampt], [PT])
                        k.tt("dve", PT[:, 0:w], PT[:, 0:w], cm[:, j, 0:w], ALU.mult, [PT, cm], [PT])
                    else:
                        k.act(PT[:, 0:w], Sb[:, 0:w], AF.Exp, [Sb, nck], [PT], bias=nck[:, kb, h:h + 1])
                    cur = PT
                if prev is not None:
                    pkb = kb - 1
                    first, last = (pkb == 0), (pkb == nkb - 1)
                    k.mm(O[:, 0:w], vb[:, pkb, :], prev[:, 0:w], first, last, [vb, prev], [O])
                    k.mm(Z[:, 0:w], (vones if pkb == 0 else ones)[:], prev[:, 0:w], first, last, [vones, ones, prev], [Z])
                prev = cur
            k.ts("dve", rz[:, 0:w], Z[:, 0:w], tiny[:, 0:1], None, ALU.add, None, [Z, tiny], [rz])
            k.recip(rz[:, 0:w], rz[:, 0:w], [rz], [rz])
            lo = max(c0, PAD)
            if lo >= c0 + w:
                continue
            off = lo - c0
            Y = yo[yi % 2]
            yi += 1
            k.tt("dve", Y[:, 0:w], O[:, 0:w], rz[:, 0:w], ALU.mult, [O, rz], [Y])
            k.dma("sp", yT, yT.ap()[h * 128:(h + 1) * 128, lo - PAD:c0 + w - PAD], Y, Y[:, off:w], sem_tile=Y)
    return k.emit()


PAD = 112
GN_EPS = 64e-5
CDEC = math.exp(-0.5)


def rwkv_param_cols(NG):
    names = ["mu_r", "mu_k", "mu_v", "w0", "a0", "k_k", "k_a", "r_k"]
    cols = {n: i * NG for i, n in enumerate(names)}
    b = len(names) * NG
    cols["mu_w"] = b
    cols["mu_a"] = b + 1
    cols["mu_g"] = b + 2
    return cols, b + 6


def build_rwkv(T_real=8208, NG=4, stop=99, stop_tile=0):
    k = K()
    TP = PAD + T_real
    assert TP % 128 == 0 and (TP - 128) % 512 == 0
    CW = NG * 128
    NU = 2 * NG
    cols, NPAR = rwkv_param_cols(NG)
    zr = k.dram("zr", [CW, TP], F32, kind="ExternalInput")
    zk = k.dram("zk", [CW, TP], F32, kind="ExternalInput")
    zv = k.dram("zv", [CW, TP], F32, kind="ExternalInput")
    zw = k.dram("zw", [128, TP], F32, kind="ExternalInput")
    za = k.dram("za", [128, TP], F32, kind="ExternalInput")
    zg = k.dram("zg", [512, TP], F32, kind="ExternalInput")
    rp = k.dram("rp", [128, NPAR], F32, kind="ExternalInput")
    w_up = k.dram("w_up", [128, CW], F32, kind="ExternalInput")
    a_up = k.dram("a_up", [128, CW], F32, kind="ExternalInput")
    g_up = k.dram("g_up", [128, 4, CW], F32, kind="ExternalInput")
    gnw = k.dram("gnw", [128, CW], F32, kind="ExternalInput")
    gnb = k.dram("gnb", [128, CW], F32, kind="ExternalInput")
    cmaskc = k.dram("cmaskc", [128, 512], F32, kind="ExternalInput")
    m22 = k.dram("m22", [128, 512], F32, kind="ExternalInput")
    mst = k.dram("mst", [128, 128], F32, kind="ExternalInput")
    ident = k.dram("ident", [128, 128], F32, kind="ExternalInput")
    bones = k.dram("bones", [128, 128], F32, kind="ExternalInput")
    hsel = k.dram("hsel", [128, 2], F32, kind="ExternalInput")
    lmk = k.dram("lmk", [128, 7, 128], F32, kind="ExternalInput")
    lmkt = k.dram("lmkt", [128, 7, 128], F32, kind="ExternalInput")
    y = k.dram("y", [TP, CW], F32, kind="ExternalOutput")

    def ld(dt, shape, src, eng="sp", name=None):
        t = k.sb(shape, dt, name)
        k.dma(eng, t, t[:], src, src.ap())
        return t
    par = ld(F32, [128, NPAR], rp, name="par")
    wupb = ld(BF16, [128, CW], w_up, "pool", "wupb")
    aupb = ld(BF16, [128, CW], a_up, "pool", "aupb")
    gupb = ld(BF16, [128, 4, CW], g_up, "pool", "gupb")
    gnw_s = ld(F32, [128, CW], gnw, name="gnw_s")
    gnb_s = ld(F32, [128, CW], gnb, name="gnb_s")
    cmk = ld(F32, [128, 512], cmaskc, name="cmk")
    M22 = ld(BF16, [128, 512], m22, "pool", "M22")
    MST = ld(BF16, [128, 128], mst, "pool", "MST")
    IDF = ld(F32, [128, 128], ident, name="IDF")
    IDB = ld(BF16, [128, 128], ident, "pool", "IDB")
    BON = ld(BF16, [128, 128], bones, "pool", "BON")
    HSEL = ld(BF16, [128, 2], hsel, "pool", "HSEL")
    HM = ld(F32, [128, 2], hsel, name="HM")
    LMK = ld(BF16, [128, 7, 128], lmk, "pool", "LMK")
    LMKT = ld(BF16, [128, 7, 128], lmkt, "pool", "LMKT")
    omka = k.sb([128, NG], F32, "omka")
    k.ts("dve", omka[:], par[:, cols["k_a"]:cols["k_a"] + NG], -1.0, 1.0, ALU.mult, ALU.add, [par], [omka])
    epsg = k.sb([128, 1], F32, "epsg")
    k.memset("pool", epsg, epsg[:], GN_EPS)

    def pc(name, g=0):
        c = cols[name] + g
        return par[:, c:c + 1]

    banks = [k.ps([128, 512], F32, f"bank{i}") for i in range(8)]

    class PS:
        def __init__(self):
            self.cur = 0
            self.off = 0
            self.pending = []

        def alloc(self, ncols):
            if self.off + ncols > 512:
                self.flush()
            b_, o_ = banks[self.cur], self.off
            self.off += ncols
            return b_, o_

        def later(self, fn):
            self.pending.append(fn)

        def flush(self):
            for fn in self.pending:
                fn()
            self.pending = []
            if self.off > 0:
                self.cur = (self.cur + 1) % 8
                self.off = 0
    ps = PS()

    W = 512
    def f32t(n):
        return k.sb([128, W], F32, n)
    Xs = {n: k.sb([128, W + 1], F32, "X" + n) for n in ("r", "k", "v", "w", "a", "g")}
    sh = {n: f32t("sh" + n) for n in ("r", "k", "v", "w", "a", "g")}
    dtmp = f32t("dtmp")
    th = k.sb([128, W], BF16, "th")
    adb = k.sb([128, W], BF16, "adb")
    sg = k.sb([128, 4, W], BF16, "sg")
    lw, aa, kq, nrm, kk, k32, bv, cs, d1, d2 = [f32t(n) for n in ("lw", "aa", "kq", "nrm", "kk", "k32", "bv", "cs", "d1", "d2")]
    Eprev, Ecs, Eneg, Eend = [f32t(n) for n in ("Eprev", "Ecs", "Eneg", "Eend")]
    sqb = k.sb([128, W], BF16, "sqb")
    Khb = k.sb([128, W], BF16, "Khb")
    Bhb = k.sb([128, W], BF16, "Bhb")
    vbf = k.sb([128, W], BF16, "vbf")
    rkp = k.sb([128, W], BF16, "rkp")
    rkf = f32t("rkf")
    AR = [k.sb([128, 4, 2, 128], BF16, f"AR{g}") for g in range(NG)]
    ARm = [[k.sb([128, 4, 2, 128], BF16, f"ARm{g}_{h}") for h in range(2)] for g in range(NG)]
    Kt = [k.sb([128, W], BF16, f"Kt{g}") for g in range(NG)]
    Bt = [k.sb([128, W], BF16, f"Bt{g}") for g in range(NG)]
    KhT = [k.sb([128, 4, 128], BF16, f"KhT{g}") for g in range(NG)]
    BhT = [k.sb([128, 4, 128], BF16, f"BhT{g}") for g in range(NG)]
    VT = [k.sb([128, 4, 128], BF16, f"VT{g}") for g in range(NG)]
    gam = [k.sb([128, 4], F32, f"gam{g}") for g in range(NG)]
    rk = [k.sb([128, 4, 2], F32, f"rk{g}") for g in range(NG)]
    gtm = k.sb([128, 4, CW], F32, "gtm")
    PTR = k.ps([128, 3, 128], BF16, "ptr") if False else None

    S32 = [k.sb([128, 64], F32, f"S32_{g}") for g in range(NG)]
    Sb = [k.sb([128, 64], BF16, f"Sb_{g}") for g in range(NG)]
    S32v, Sbv = [], []
    for g in range(NG):
        k.memset("pool", S32[g], S32[g][:], 0.0)
        k.memset("pool", Sb[g], Sb[g][:], 0.0)
        S32v.append([Tile(k, S32[g].h, f"S32v{g}_{h}") for h in range(2)])
        Sbv.append([Tile(k, Sb[g].h, f"Sbv{g}_{h}") for h in range(2)])
        for h in range(2):
            S32v[g][h].last_write = S32[g].last_write
            Sbv[g][h].last_write = Sb[g].last_write
        k.tiles.extend(S32v[g] + Sbv[g])
    UU = [k.sb([128, 512], BF16, f"UU{u}") for u in range(NU)]
    Qb = [[k.sb([128, 128], BF16, f"Qb{u}_{i}") for i in range(1)] for u in range(NU)]
    NPAIR = NU // 2
    XX = [k.sb([128, 4, 128], BF16, f"XX{p}") for p in range(NPAIR)]
    TE32 = [k.sb([128, 4, 128], F32, f"TE32_{p}") for p in range(NPAIR)]
    TEb = [k.sb([128, 4, 128], BF16, f"TEb{p}") for p in range(NPAIR)]
    ID4F = k.sb([128, 4, 128], F32, "ID4F")
    for i_ in range(4):
        k.copy("act", ID4F[:, i_, :], IDF[:], [IDF], [ID4F])
    NLV = 7
    Uoff = [k.sb([128, NLV, 128], BF16, f"Uoff{u}") for u in range(NU)]
    UoffT = [k.sb([128, NLV, 128], BF16, f"UoffT{u}") for u in range(NU)]
    WT = [k.sb([128, 64], BF16, f"WT{u}") for u in range(NU)]
    ZT = [k.sb([128, 64], BF16, f"ZT{u}") for u in range(NU)]
    Yc = [k.sb([128, CW], F32, f"Yc{i}") for i in range(2)]
    Dn = k.sb([128, CW], F32, "Dn")
    Dsq = k.sb([128, CW], F32, "Dsq")
    stmp = k.sb([128, 64], F32, "stmp")
    stat = [k.sb([128, NU], F32, f"stat{i}") for i in range(3)]
    oacc = [k.sb([128, CW], F32, f"oacc{i}") for i in range(1)]

    qtiles = [(0, 128)] + [(128 + 512 * i, 512) for i in range((TP - 128) // 512)]
    ychunk = 0
    stop_req = stop
    for ti_, (c0, w) in enumerate(qtiles):
        nch = w // 128
        stop = stop_req if ti_ == stop_tile else 99

        def load_shift(name, src, rows, mu_ap, eng="sp"):
            X = Xs[name]
            if c0 == 0:
                k.memset("pool", X, X[:, 0:1], 0.0)
                k.dma(eng, X, X[:, 1:w + 1], src, src.ap()[rows, 0:w])
            else:
                k.dma(eng, X, X[:, 0:w + 1], src, src.ap()[rows, c0 - 1:c0 + w])
            o = sh[name]
            k.tt("pool", dtmp[:, 0:w], X[:, 0:w], X[:, 1:w + 1], ALU.subtract, [X], [dtmp])
            k.stt(o[:, 0:w], dtmp[:, 0:w], mu_ap, X[:, 1:w + 1], ALU.mult, ALU.add, [dtmp, par, X], [o])
            return o

        o = load_shift("w", zw, slice(0, 128), pc("mu_w"))
        k.act(th[:, 0:w], o[:, 0:w], AF.Tanh, [o], [th])
        o = load_shift("a", za, slice(0, 128), pc("mu_a"))
        k.copy("act", adb[:, 0:w], o[:, 0:w], [o], [adb])
        for kc in range(4):
            o = load_shift("g", zg, slice(kc * 128, (kc + 1) * 128), pc("mu_g", kc))
            k.act(sg[:, kc, 0:w], o[:, 0:w], AF.Sigmoid, [o], [sg])
        for j in range(nch):
            pt_, o_ = ps.alloc(CW)
            for kc in range(4):
                k.mm(pt_[:, o_:o_ + CW], sg[:, kc, j * 128:(j + 1) * 128], gupb[:, kc, :], kc == 0, kc == 3, [sg, gupb], [pt_])
            ps.later(lambda pt_=pt_, o_=o_, j=j: k.copy("act", gtm[:, j, :], pt_[:, o_:o_ + CW], [pt_], [gtm]))
        ps.flush()
        if stop <= 1:
            return k.emit()

        for g in range(NG):
            rows = slice(g * 128, (g + 1) * 128)
            r_ = load_shift("r", zr, rows, pc("mu_r", g))
            k_ = load_shift("k", zk, rows, pc("mu_k", g))
            v_ = load_shift("v", zv, rows, pc("mu_v", g))
            pa, oa = ps.alloc(w)
            k.mm(pa[:, oa:oa + w], wupb[:, rows], th[:, 0:w], True, True, [wupb, th], [pa])
            ps.later(lambda pa=pa, oa=oa, g=g: k.act(lw[:, 0:w], pa[:, oa:oa + w], AF.Sigmoid, [pa, par], [lw], bias=pc("w0", g)))
            ps.flush()
            pa, oa = ps.alloc(w)
            k.mm(pa[:, oa:oa + w], aupb[:, rows], adb[:, 0:w], True, True, [aupb, adb], [pa])
            ps.later(lambda pa=pa, oa=oa, g=g: k.act(aa[:, 0:w], pa[:, oa:oa + w], AF.Sigmoid, [pa, par], [aa], bias=pc("a0", g)))
            ps.flush()
            k.ts("dve", kq[:, 0:w], k_[:, 0:w], pc("k_k", g), None, ALU.mult, None, [k_, par], [kq])
            k.act(sqb[:, 0:w], kq[:, 0:w], AF.Square, [kq], [sqb])
            pa, oa = ps.alloc(w)
            k.mm(pa[:, oa:oa + w], BON[:], sqb[:, 0:w], True, True, [BON, sqb], [pa])
            ps.later(lambda pa=pa, oa=oa: k.act(nrm[:, 0:w], pa[:, oa:oa + w], AF.Sqrt, [pa], [nrm]))
            ps.flush()
            k.ts("dve", nrm[:, 0:w], nrm[:, 0:w], 1e-12, None, ALU.max, None, [nrm], [nrm])
            k.recip(nrm[:, 0:w], nrm[:, 0:w], [nrm], [nrm])
            k.tt("dve", kk[:, 0:w], kq[:, 0:w], nrm[:, 0:w], ALU.mult, [kq, nrm], [kk])
            k.ts("pool", k32[:, 0:w], aa[:, 0:w], pc("k_a", g), omka[:, g:g + 1], ALU.mult, ALU.add, [aa, par, omka], [k32])
            k.tt("pool", k32[:, 0:w], k32[:, 0:w], k_[:, 0:w], ALU.mult, [k32, k_], [k32])
            k.tt("pool", bv[:, 0:w], kk[:, 0:w], aa[:, 0:w], ALU.mult, [kk, aa], [bv])
            k.op("dve", lambda e, w=w: e.tensor_tensor_scan(out=cs[:, 0:w], data0=cmk[:, 0:w], data1=lw[:, 0:w],
                                                            initial=0.0, op0=ALU.mult, op1=ALU.add), [cmk, lw], [cs])
            k.tt("pool", d1[:, 0:w], cs[:, 0:w], lw[:, 0:w], ALU.subtract, [cs, lw], [d1])
            for j in range(nch):
                sl = slice(j * 128, (j + 1) * 128)
                k.ts("dve", d2[:, sl], cs[:, sl], cs[:, j * 128 + 127:j * 128 + 128], None, ALU.subtract, None, [cs], [d2])
            k.act(Eprev[:, 0:w], d1[:, 0:w], AF.Exp, [d1], [Eprev], scale=-CDEC)
            k.act(Ecs[:, 0:w], cs[:, 0:w], AF.Exp, [cs], [Ecs], scale=-CDEC)
            k.act(Eneg[:, 0:w], cs[:, 0:w], AF.Exp, [cs], [Eneg], scale=CDEC)
            k.act(Eend[:, 0:w], d2[:, 0:w], AF.Exp, [d2], [Eend], scale=CDEC)
            ARg = AR[g]
            k.stt(ARg[:, 0:nch, 0, :], kk[:, 0:w].rearrange("p (c t) -> p c t", t=128), -1.0,
                  Eprev[:, 0:w].rearrange("p (c t) -> p c t", t=128), ALU.mult, ALU.mult, [kk, Eprev], [ARg])
            k.tt("dve", ARg[:, 0:nch, 1, :], r_[:, 0:w].rearrange("p (c t) -> p c t", t=128),
                 Ecs[:, 0:w].rearrange("p (c t) -> p c t", t=128), ALU.mult, [r_, Ecs], [ARg])
            for hh in range(2):
                k.act(ARm[g][hh][:, 0:nch, :, :].rearrange("p c a t -> p (c a t)"), ARg[:, 0:nch, :, :].rearrange("p c a t -> p (c a t)"),
                      AF.Copy, [ARg, HM], [ARm[g][hh]], scale=HM[:, hh:hh + 1])
            k.tt("pool", Kt[g][:, 0:w], k32[:, 0:w], Eneg[:, 0:w], ALU.mult, [k32, Eneg], [Kt[g]])
            k.tt("pool", Bt[g][:, 0:w], bv[:, 0:w], Eneg[:, 0:w], ALU.mult, [bv, Eneg], [Bt[g]])
            k.tt("dve", Khb[:, 0:w], k32[:, 0:w], Eend[:, 0:w], ALU.mult, [k32, Eend], [Khb])
            k.tt("dve", Bhb[:, 0:w], bv[:, 0:w], Eend[:, 0:w], ALU.mult, [bv, Eend], [Bhb])
            k.copy("act", vbf[:, 0:w], v_[:, 0:w], [v_], [vbf])
            for j in range(nch):
                k.copy("act", gam[g][:, j:j + 1], Ecs[:, j * 128 + 127:j * 128 + 128], [Ecs], [gam[g]])
            k.stt(rkf[:, 0:w], r_[:, 0:w], pc("r_k", g), k32[:, 0:w], ALU.mult, ALU.mult, [r_, par, k32], [rkf])
            k.copy("act", rkp[:, 0:w], rkf[:, 0:w], [rkf], [rkp])
            for j in range(nch):
                sl = slice(j * 128, (j + 1) * 128)
                for (srcb, dstT) in ((Khb, KhT[g]), (Bhb, BhT[g]), (vbf, VT[g])):
                    p_, o_ = ps.alloc(128)
                    k.mm(p_[:, o_:o_ + 128], srcb[:, sl], IDB[:], True, True, [srcb, IDB], [p_])
                    ps.later(lambda p_=p_, o_=o_, dstT=dstT, j=j: k.copy("act", dstT[:, j, :], p_[:, o_:o_ + 128], [p_], [dstT]))
                p_, o_ = ps.alloc(2)
                k.mm(p_[:, o_:o_ + 2], rkp[:, sl], HSEL[:], True, True, [rkp, HSEL], [p_])
                ps.later(lambda p_=p_, o_=o_, g=g, j=j: k.copy("act", rk[g][:, j, :], p_[:, o_:o_ + 2], [p_], [rk[g]]))
            ps.flush()

        if stop <= 2:
            return k.emit()
        for j in range(nch):
            sl = slice(j * 128, (j + 1) * 128)
            units = [(g, hh) for g in range(NG) for hh in range(2)]
            for u, (g, hh) in enumerate(units):
                arv = ARm[g][hh][:, j, :, :].rearrange("p a t -> p (a t)")
                pa, oa = ps.alloc(256)
                k.mm(pa[:, oa:oa + 256], Bt[g][:, sl], arv, True, True, [Bt[g], ARm[g][hh]], [pa])
                ps.later(lambda pa=pa, oa=oa, u=u: k.tt("dve", UU[u][:, 0:256], pa[:, oa:oa + 256], M22[:, 0:256], ALU.mult, [pa, M22], [UU[u]]))
                pb_, ob = ps.alloc(256)
                k.mm(pb_[:, ob:ob + 256], Kt[g][:, sl], arv, True, True, [Kt[g], ARm[g][hh]], [pb_])
                ps.later(lambda pb_=pb_, ob=ob, u=u: k.tt("dve", UU[u][:, 256:512], pb_[:, ob:ob + 256], M22[:, 256:512], ALU.mult, [pb_, M22], [UU[u]]))
            ps.flush()
            if stop <= 2.1:
                return k.emit()
            for u, (g, hh) in enumerate(units):
                pq, oq = ps.alloc(128)
                k.mm(pq[:, oq:oq + 128], ARm[g][hh][:, j, 0, :], Bt[g][:, sl], True, True, [ARm[g][hh], Bt[g]], [pq])
                ps.later(lambda pq=pq, oq=oq, u=u: k.tt("dve", Qb[u][0][:], pq[:, oq:oq + 128], MST[:], ALU.mult, [pq, MST], [Qb[u][0]]))
            ps.flush()
            for u in range(NU):
                k.tt("dve", Uoff[u][:], UU[u][:, 0:128].unsqueeze(1).to_broadcast([128, NLV, 128]), LMK[:], ALU.mult, [UU[u], LMK], [Uoff[u]])
                k.tt("dve" if u % 2 else "pool", UoffT[u][:], Qb[u][0][:].unsqueeze(1).to_broadcast([128, NLV, 128]), LMKT[:], ALU.mult, [Qb[u][0], LMKT], [UoffT[u]])
            for p in range(NPAIR):
                k.copy("act", TE32[p][:], ID4F[:], [ID4F], [TE32[p]])
                k.copy("act", TEb[p][:], ID4F[:], [ID4F], [TEb[p]])
            for lv in range(NLV):
                for p in range(NPAIR):
                    px, ox = ps.alloc(512)
                    for q_ in range(2):
                        u = 2 * p + q_
                        k.mm(px[:, ox + 256 * q_:ox + 256 * q_ + 128], UoffT[u][:, lv, :], TEb[p][:, 2 * q_, :], True, True, [UoffT[u], TEb[p]], [px])
                        k.mm(px[:, ox + 256 * q_ + 128:ox + 256 * q_ + 256], Uoff[u][:, lv, :], TEb[p][:, 2 * q_ + 1, :], True, True, [Uoff[u], TEb[p]], [px])
                    ps.later(lambda px=px, ox=ox, p=p: k.copy("act", XX[p][:].rearrange("p a b -> p (a b)"), px[:, ox:ox + 512], [px], [XX[p]]))
                ps.flush()
                for p in range(NPAIR):
                    py_, oy_ = ps.alloc(512)
                    for q_ in range(2):
                        u = 2 * p + q_
                        k.mm(py_[:, oy_ + 256 * q_:oy_ + 256 * q_ + 128], TEb[p][:, 2 * q_ + 1, :], XX[p][:, 2 * q_, :], True, True, [TEb[p], XX[p]], [py_])
                        k.mm(py_[:, oy_ + 256 * q_ + 128:oy_ + 256 * q_ + 256], TEb[p][:, 2 * q_, :], XX[p][:, 2 * q_ + 1, :], True, True, [TEb[p], XX[p]], [py_])

                    def upd(py_=py_, oy_=oy_, p=p):
                        k.tt("dve", TE32[p][:].rearrange("p a b -> p (a b)"), TE32[p][:].rearrange("p a b -> p (a b)"), py_[:, oy_:oy_ + 512], ALU.add, [TE32[p], py_], [TE32[p]])
                        k.copy("act", TEb[p][:], TE32[p][:], [TE32[p]], [TEb[p]])
                    ps.later(upd)
                ps.flush()
            if stop <= 3:
                return k.emit()
            spend = []
            Y = Yc[ychunk % 2]
            for u, (g, hh) in enumerate(units):
                hs = slice(hh * 64, hh * 64 + 64)
                vcols = VT[g][:, j, hh * 64:hh * 64 + 64]
                pw, ow = ps.alloc(64)
                k.mm(pw[:, ow:ow + 64], ARm[g][hh][:, j, 0, :], Sb[g][:], True, False, [ARm[g][hh], Sb[g]], [pw])
                k.mm(pw[:, ow:ow + 64], UU[u][:, 256:384], vcols, False, True, [UU[u], VT[g]], [pw])
                ps.later(lambda pw=pw, ow=ow, u=u: k.copy("act", WT[u][:], pw[:, ow:ow + 64], [pw], [WT[u]]))
            ps.flush()
            if stop <= 3.2:
                return k.emit()
            for u, (g, hh) in enumerate(units):
                pz, oz = ps.alloc(64)
                k.mm(pz[:, oz:oz + 64], TEb[u // 2][:, 2 * (u % 2), :], WT[u][:], True, True, [TEb[u // 2], WT[u]], [pz])
                ps.later(lambda pz=pz, oz=oz, u=u: k.copy("dve", ZT[u][:], pz[:, oz:oz + 64], [pz], [ZT[u]]))
            ps.flush()
            if stop <= 3.4:
                return k.emit()
            for u, (g, hh) in enumerate(units):
                hs = slice(hh * 64, hh * 64 + 64)
                vcols = VT[g][:, j, hh * 64:hh * 64 + 64]
                py, oy = ps.alloc(64)
                k.mm(py[:, oy:oy + 64], ARm[g][hh][:, j, 1, :], Sb[g][:], True, False, [ARm[g][hh], Sb[g]], [py])
                k.mm(py[:, oy:oy + 64], UU[u][:, 128:256], ZT[u][:], False, False, [UU[u], ZT[u]], [py])
                k.mm(py[:, oy:oy + 64], UU[u][:, 384:512], vcols, False, True, [UU[u], VT[g]], [py])
                ycol = g * 128 + hh * 64
                ps.later(lambda py=py, oy=oy, ycol=ycol, Y=Y: k.copy("act", Y[:, ycol:ycol + 64], py[:, oy:oy + 64], [py], [Y]))
            ps.flush()
            for u, (g, hh) in enumerate(units):
                vcols = VT[g][:, j, hh * 64:hh * 64 + 64]
                ps_, os_ = ps.alloc(64)
                k.mm(ps_[:, os_:os_ + 64], BhT[g][:, j, :], ZT[u][:], True, False, [BhT[g], ZT[u]], [ps_])
                k.mm(ps_[:, os_:os_ + 64], KhT[g][:, j, :], vcols, False, True, [KhT[g], VT[g]], [ps_])

                spend.append((ps_, os_))
                if hh == 1:
                    def supd(sp=list(spend), g=g, j=j):
                        (p0, o0), (p1, o1) = sp
                        k.ts("dve", stmp[:], p0[:, o0:o0 + 64], HM[:, 0:1], None, ALU.mult, None, [p0, HM], [stmp])
                        k.stt(S32[g][:], S32[g][:], gam[g][:, j:j + 1], stmp[:], ALU.mult, ALU.add, [S32[g], gam[g], stmp], [S32[g]])
                        k.stt(S32[g][:], p1[:, o1:o1 + 64], HM[:, 1:2], S32[g][:], ALU.mult, ALU.add, [p1, HM, S32[g]], [S32[g]])
                        k.copy("pool", Sb[g][:], S32[g][:], [S32[g]], [Sb[g]])
                    ps.later(supd)
                    spend.clear()
            ps.flush()
            if stop <= 4:
                return k.emit()
            st0, st1, st2 = stat
            k.op("dve", lambda e, Y=Y: e.tensor_reduce(out=st0[:], in_=Y[:].rearrange("p (h c) -> p h c", c=64),
                                                       axis=AX.X, op=ALU.add), [Y], [st0])
            k.ts("dve", st0[:], st0[:], 1.0 / 64, None, ALU.mult, None, [st0], [st0])
            for u in range(NU):
                hc = slice(u * 64, (u + 1) * 64)
                k.ts("dve", Dn[:, hc], Y[:, hc], st0[:, u:u + 1], None, ALU.subtract, None, [Y, st0], [Dn])
            k.act(Dsq[:], Dn[:], AF.Square, [Dn], [Dsq])
            k.op("dve", lambda e: e.tensor_reduce(out=st1[:], in_=Dsq[:].rearrange("p (h c) -> p h c", c=64),
                                                  axis=AX.X, op=ALU.add), [Dsq], [st1])
            k.act(st2[:], st1[:], AF.Sqrt, [st1, epsg], [st2], scale=1.0 / 64, bias=epsg[:])
            k.recip(st2[:], st2[:], [st2], [st2])
            O = oacc[0]
            for u, (g, hh) in enumerate(units):
                hc = slice(u * 64, (u + 1) * 64)
                k.stt(O[:, hc], Dn[:, hc], st2[:, u:u + 1], gnw_s[:, hc], ALU.mult, ALU.mult, [Dn, st2, gnw_s], [O])
            k.tt("pool", O[:], O[:], gnb_s[:], ALU.add, [O, gnb_s], [O])
            for u, (g, hh) in enumerate(units):
                hc = slice(u * 64, (u + 1) * 64)
                k.stt(O[:, hc], VT[g][:, j, hh * 64:hh * 64 + 64], rk[g][:, j, hh:hh + 1], O[:, hc], ALU.mult, ALU.add,
                      [VT[g], rk[g], O], [O])
            k.tt("dve", O[:], O[:], gtm[:, j, :], ALU.mult, [O, gtm], [O])
            t0 = c0 + j * 128
            k.dma("sp", y, y.ap()[t0:t0 + 128, :], O, O[:], sem_tile=O)
            ychunk += 1
        if stop <= 5:
            return k.emit()
    return k.emit()


D_MODEL = 4096
N_META = 16
SEQ = 8192
T_REAL = N_META + SEQ
TPAD = PAD + T_REAL
NCORE = 8
NT = 2 * T_REAL // NCORE
ROPE_THETA = 500000.0

_prog_cache = {}


def _prog(key, fn):
    if key not in _prog_cache:
        _prog_cache[key] = fn()
    return _prog_cache[key]


def _run(nc, in_maps):
    res = run_bass_kernel_spmd(nc, in_maps, core_ids=list(range(NCORE)))
    return res.results


def _glay(g):
    return np.ascontiguousarray(np.asarray(g, np.float32).reshape(-1, 128).T)


def _pad_cols(xT):
    o = np.zeros((xT.shape[0], TPAD), np.float32)
    o[:, PAD:] = xT
    return o


def _tokmajor_blocks(xT_pad):
    C = xT_pad.shape[0]
    return np.ascontiguousarray(xT_pad.T.reshape(TPAD // 128, 128, C).transpose(1, 0, 2))


def _attn_consts():
    pos = np.arange(TPAD) - PAD
    inv = (ROPE_THETA ** (-np.arange(16, dtype=np.float32) / 16)).astype(np.float32)
    ang = pos.astype(np.float32)[None, :] * inv[:, None]
    rope = np.zeros((32, 2, TPAD), np.float32)
    rope[:16, 0] = np.cos(ang)
    rope[16:, 0] = np.cos(ang)
    rope[:16, 1] = np.sin(ang)
    rope[16:, 1] = np.sin(ang)
    cm = np.zeros((128, 4, 512), np.float32)
    kp = np.arange(128)[:, None]
    q = np.arange(512)[None, :]
    for j in range(4):
        cm[:, j, :] = (q >= j * 128 + kp)
    rmat = np.zeros((128, 32), np.float32)
    for p in range(16):
        rmat[p + 16, p] = -1.0
        rmat[p, p + 16] = 1.0
    valid = np.ones((128, TPAD // 128), np.float32)
    valid[:PAD, 0] = 0
    return rope, cm, rmat, valid


def _rwkv_consts():
    cm = np.ones((128, 512), np.float32)
    cm[:, ::128] = 0
    s_ = np.arange(128)[:, None]
    t_ = np.arange(128)[None, :]
    strict = (s_ < t_).astype(np.float32)
    incl = (s_ <= t_).astype(np.float32)
    m22 = np.concatenate([strict, incl, strict, incl], 1)
    mst = np.ascontiguousarray(strict.T)
    bones = np.zeros((128, 128), np.float32)
    bones[:64, :64] = 1
    bones[64:, 64:] = 1
    hsel = np.zeros((128, 2), np.float32)
    hsel[:64, 0] = 1
    hsel[64:, 1] = 1
    lmk = np.zeros((128, 7, 128), np.float32)
    sa = np.arange(128)[:, None]
    ta = np.arange(128)[None, :]
    for lv in range(7):
        m = 1 << lv
        lmk[:, lv, :] = ((sa // (2 * m) == ta // (2 * m)) & (sa // m != ta // m) & (sa < ta))
    lmkt = np.ascontiguousarray(lmk.transpose(2, 1, 0))
    return dict(cmaskc=cm, m22=m22, mst=mst, ident=np.eye(128, dtype=np.float32), bones=bones, hsel=hsel, lmk=lmk, lmkt=lmkt)


def _gather_T(results, name):
    per_b = []
    for b in range(2):
        per_b.append(np.concatenate([results[b * 4 + i][name] for i in range(4)], axis=1))
    return per_b


def _split_T(per_b):
    out = []
    for b in range(2):
        for i in range(4):
            out.append(np.ascontiguousarray(per_b[b][:, i * NT:(i + 1) * NT]))
    return out


def kernel(x, meta_tokens,
           mix_norm_0, w_in_0, conv_w_0, conv_b_0, lru_wa_0, lru_ba_0, lru_wx_0, lru_bx_0, lru_lam_0,
           diff_q_gain_0, diff_k_gain_0, diff_lq1_0, diff_lk1_0, diff_lq2_0, diff_lk2_0, diff_sub_gain_0,
           w_out_0, ffn_norm_0, ffn_up_0, ffn_down_0,
           mix_norm_1, w_in_1, rwkv_mu_1, rwkv_w0_1, rwkv_w_up_1, rwkv_a0_1, rwkv_a_up_1, rwkv_g_up_1,
           rwkv_k_k_1, rwkv_k_a_1, rwkv_r_k_1, rwkv_gn_w_1, rwkv_gn_b_1,
           fox_q_gain_1, fox_k_gain_1, fox_f_bias_1, w_out_1, ffn_norm_1, ffn_up_1, ffn_down_1):
    f32 = lambda a: np.ascontiguousarray(np.asarray(a, dtype=np.float32))
    x = np.asarray(x, np.float32)
    meta = np.asarray(meta_tokens, np.float32)
    hT_b = []
    for b in range(2):
        hT_b.append(np.concatenate([meta.T, x[b].T], axis=1))
    hT_c = _split_T(hT_b)
    del hT_b

    nc1 = _prog("g1", lambda: build_gemm(1, NIN=10240))
    w_in_0 = f32(w_in_0)
    g0 = _glay(mix_norm_0)
    r1 = _run(nc1, [dict(hT=hT_c[c], mix_g=g0, w_in=w_in_0) for c in range(NCORE)])
    U_b = _gather_T(r1, "UT")
    del r1

    lam_init = 0.8 - 0.6 * math.exp(-0.3 * 0)
    nc2 = _prog("m0", lambda: build_mix0(T_real=T_REAL, NB=4, NH=2, LW=456, lambda_init=lam_init))
    rope, cmk, rmat, valid = _attn_consts()
    conv_w_0 = f32(conv_w_0); conv_b_0 = f32(conv_b_0); lru_lam_0 = f32(lru_lam_0)
    lru_wa_0 = f32(lru_wa_0); lru_wx_0 = f32(lru_wx_0); lru_ba_0 = f32(lru_ba_0); lru_bx_0 = f32(lru_bx_0)
    subg = f32(diff_sub_gain_0)
    att_p = np.ascontiguousarray(np.stack([f32(diff_q_gain_0), f32(diff_k_gain_0), f32(diff_lq1_0), f32(diff_lk1_0),
                                           f32(diff_lq2_0), f32(diff_lk2_0), subg[:128], subg[128:]], axis=1))
    ims = []
    for c in range(NCORE):
        b, g = c // 4, c % 4
        U = U_b[b]
        lru_p = np.zeros((128, 4, 8), np.float32)
        lru_w = np.zeros((128, 4, 2, 128), np.float32)
        for n in range(4):
            gi = g * 4 + n
            sl = slice(gi * 128, (gi + 1) * 128)
            for j in range(4):
                lru_p[:, n, j] = conv_w_0[j, sl]
            lru_p[:, n, 4] = conv_b_0[sl]
            lru_p[:, n, 5] = lru_ba_0[gi]
            lru_p[:, n, 6] = lru_bx_0[gi]
            lru_p[:, n, 7] = lru_lam_0[sl]
            lru_w[:, n, 0] = lru_wa_0[gi]
            lru_w[:, n, 1] = lru_wx_0[gi]
        r0 = g * 512
        ims.append(dict(
            xbT=_pad_cols(U[r0:r0 + 512]), gbT=_pad_cols(U[2048 + r0:2048 + r0 + 512]),
            lru_p=lru_p, lru_w=lru_w,
            qT=_pad_cols(U[4096 + r0:4096 + r0 + 512]), kT=_pad_cols(U[6144 + r0:6144 + r0 + 512]),
            vtm=_tokmajor_blocks(_pad_cols(U[8192 + r0:8192 + r0 + 512])),
            att_p=att_p, rope=rope, cmask=cmk, rmat=rmat, valid=valid))
    del U_b
    r2 = _run(nc2, ims)
    del ims
    yT_b = []
    for b in range(2):
        ya = np.concatenate([r2[b * 4 + g]["yT"][0:512] for g in range(4)], axis=0)
        yb = np.concatenate([r2[b * 4 + g]["yT"][512:1024] for g in range(4)], axis=0)
        yT_b.append(np.concatenate([ya, yb], axis=0))
    del r2
    yT_c = _split_T(yT_b)
    del yT_b

    nc3 = _prog("g3", lambda: build_gemm(3, NIN=13040))
    w3 = dict(w_out=f32(w_out_0), ffn_g=_glay(ffn_norm_0), w_up=f32(ffn_up_0), w_down=f32(ffn_down_0),
              mix_g=_glay(mix_norm_1), w_in=f32(w_in_1))
    r3 = _run(nc3, [dict(hT=hT_c[c], yT=yT_c[c], **w3) for c in range(NCORE)])
    del w3, yT_c
    h2_c = [r3[c]["hout"] for c in range(NCORE)]
    U1_b = _gather_T(r3, "UT")
    del r3, hT_c

    nc4 = _prog("rw", lambda: build_rwkv(T_real=T_REAL, NG=4))
    cols, NPAR = rwkv_param_cols(4)
    mu = f32(rwkv_mu_1); w0 = f32(rwkv_w0_1); a0 = f32(rwkv_a0_1); k_k = f32(rwkv_k_k_1); k_a = f32(rwkv_k_a_1)
    r_k = f32(rwkv_r_k_1).reshape(-1); w_up = f32(rwkv_w_up_1); a_up = f32(rwkv_a_up_1); g_up = f32(rwkv_g_up_1)
    gn_w = f32(rwkv_gn_w_1); gn_b = f32(rwkv_gn_b_1)
    rc = _rwkv_consts()
    ims = []
    for c in range(NCORE):
        b, g = c // 4, c % 4
        U = U1_b[b]
        cs_ = slice(g * 512, (g + 1) * 512)
        rp = np.zeros((128, NPAR), np.float32)
        for name, arr in (("mu_r", mu[0:2048][cs_]), ("mu_k", mu[2048:4096][cs_]), ("mu_v", mu[4096:6144][cs_]),
                          ("w0", w0[cs_]), ("a0", a0[cs_]), ("k_k", k_k[cs_]), ("k_a", k_a[cs_]), ("r_k", r_k[cs_])):
            rp[:, cols[name]:cols[name] + 4] = arr.reshape(4, 128).T
        rp[:, cols["mu_w"]] = mu[6144:6272]
        rp[:, cols["mu_a"]] = mu[6272:6400]
        mg = np.zeros(512, np.float32)
        mg[:480] = mu[6400:6880]
        rp[:, cols["mu_g"]:cols["mu_g"] + 4] = mg.reshape(4, 128).T
        zg = np.zeros((512, TPAD), np.float32)
        zg[:480, PAD:] = U[6400:6880]
        gu = np.zeros((512, 512), np.float32)
        gu[:480] = g_up[:, cs_]
        d = dict(zr=_pad_cols(U[0:2048][cs_]), zk=_pad_cols(U[2048:4096][cs_]), zv=_pad_cols(U[4096:6144][cs_]),
                 zw=_pad_cols(U[6144:6272]), za=_pad_cols(U[6272:6400]), zg=zg, rp=rp,
                 w_up=np.ascontiguousarray(w_up[:, cs_]), a_up=np.ascontiguousarray(a_up[:, cs_]),
                 g_up=np.ascontiguousarray(gu.reshape(4, 128, 512).transpose(1, 0, 2)),
                 gnw=np.ascontiguousarray(np.broadcast_to(gn_w[cs_][None], (128, 512))),
                 gnb=np.ascontiguousarray(np.broadcast_to(gn_b[cs_][None], (128, 512))))
        d.update(rc)
        ims.append(d)
    r4 = _run(nc4, ims)
    del ims
    yc_parts = [r4[c]["y"][PAD:].T for c in range(NCORE)]
    del r4

    nc5 = _prog("fx", lambda: build_fox(T_real=T_REAL, NH=4))
    fqg = f32(fox_q_gain_1); fkg = f32(fox_k_gain_1); fb = f32(fox_f_bias_1)
    ident = np.eye(128, dtype=np.float32)
    ims = []
    for c in range(NCORE):
        b, g = c // 4, c % 4
        U = U1_b[b]
        r0 = 6880 + g * 512
        fox_p = np.zeros((128, 4), np.float32)
        fox_p[:, 0] = fqg
        fox_p[:, 1] = fkg
        fox_p[:4, 2] = fb[4 * g:4 * g + 4]
        f0 = 6880 + 6144 + 4 * g
        ims.append(dict(qT=_pad_cols(U[r0:r0 + 512]), kT=_pad_cols(U[r0 + 2048:r0 + 2048 + 512]),
                        vtm=_tokmajor_blocks(_pad_cols(U[r0 + 4096:r0 + 4096 + 512])),
                        fT=_pad_cols(U[f0:f0 + 4]), fox_p=fox_p, cmask=cmk, valid=valid, ident=ident))
    del U1_b
    r5 = _run(nc5, ims)
    del ims
    y1T_b = []
    for b in range(2):
        yc = np.concatenate([yc_parts[b * 4 + g] for g in range(4)], axis=0)
        yd = np.concatenate([r5[b * 4 + g]["yT"] for g in range(4)], axis=0)
        y1T_b.append(np.concatenate([yc, yd], axis=0))
    del r5, yc_parts
    y1T_c = _split_T(y1T_b)
    del y1T_b

    nc6 = _prog("g5", lambda: build_gemm(5))
    w6 = dict(w_out=f32(w_out_1), ffn_g=_glay(ffn_norm_1), w_up=f32(ffn_up_1), w_down=f32(ffn_down_1))
    r6 = _run(nc6, [dict(hT=h2_c[c], yT=y1T_c[c], **w6) for c in range(NCORE)])
    out_b = _gather_T(r6, "hout")
    out = np.stack([np.ascontiguousarray(out_b[b][:, N_META:].T) for b in range(2)], axis=0)
    return out.astype(np.float32)
```
